# Optimizing a Trainium2 kernel written in Bass

```python
import math
import jax, jax.numpy as jnp
from jax import lax
import numpy as np

D_MODEL = 1024
BATCH = 16
SEQ = 2048
DEPTH = 2

MEM_LEN = 256
ATTN_WIDTH = D_MODEL // 2
CONV_WIDTH_CH = D_MODEL - ATTN_WIDTH
DIFF_HEAD_DIM = 64
DIFF_HEADS = ATTN_WIDTH // (2 * DIFF_HEAD_DIM)
CONV_K = 3
CONV_GROUPS = 4
ROT_DIM = DIFF_HEAD_DIM // 4
ROPE_THETA = 500000.0
X_HEADS = 4
X_HEAD_DIM = D_MODEL // X_HEADS
D_FF = 4 * D_MODEL
Q_BLOCK = 128
NEG_INF = -1e30
NORM_EPS = 1e-6
SUBLN_EPS = 1e-5
IN_COLS = 3 * ATTN_WIDTH + 3 * CONV_WIDTH_CH

kernel_name = "hybrid_diffattn_shortconv_block"


def rms_norm(x, g, eps=NORM_EPS):
    xf = x.astype(jnp.float32)
    y = xf * lax.rsqrt(jnp.mean(xf * xf, axis=-1, keepdims=True) + eps)
    return (y * g.astype(jnp.float32)).astype(x.dtype)


def rotary_tables(positions):
    inv_freq = ROPE_THETA ** (-jnp.arange(0, ROT_DIM, 2, dtype=jnp.float32) / ROT_DIM)
    ang = positions.astype(jnp.float32)[..., None] * inv_freq
    return jnp.cos(ang), jnp.sin(ang)


def apply_partial_rotary(t, cos, sin):
    half = ROT_DIM // 2
    r1 = t[..., :half]
    r2 = t[..., half:ROT_DIM]
    rest = t[..., ROT_DIM:]
    c = cos[:, :, None, None, :].astype(t.dtype)
    s = sin[:, :, None, None, :].astype(t.dtype)
    return jnp.concatenate([r1 * c - r2 * s, r2 * c + r1 * s, rest], axis=-1)


def diff_attention(q, k, v, lam, subln_g, lambda_init):
    bsz, seq = q.shape[0], q.shape[1]
    scale = DIFF_HEAD_DIM ** -0.5
    qh = jnp.transpose(q, (0, 2, 3, 1, 4))
    kh = jnp.transpose(k, (0, 2, 3, 1, 4))
    vh = jnp.transpose(v, (0, 2, 1, 3))
    outs = []
    for start in range(0, seq, Q_BLOCK):
        end = start + Q_BLOCK
        qb = qh[:, :, :, start:end]
        kb = kh[:, :, :, :end]
        vb = vh[:, :, :end]
        s = jnp.einsum('bhcqd,bhckd->bhcqk', qb, kb).astype(jnp.float32) * scale
        mask = (start + jnp.arange(Q_BLOCK))[:, None] >= jnp.arange(end)[None, :]
        s = jnp.where(mask, s, NEG_INF)
        p = jax.nn.softmax(s, axis=-1)
        w = p[:, :, 0] - lam * p[:, :, 1]
        outs.append(jnp.einsum('bhqk,bhkd->bhqd', w.astype(vb.dtype), vb))
    o = jnp.concatenate(outs, axis=2)
    o = rms_norm(o, subln_g, eps=SUBLN_EPS) * (1.0 - lambda_init)
    return jnp.transpose(o, (0, 2, 1, 3)).reshape(bsz, seq, ATTN_WIDTH)


def short_gated_conv(b_gate, c_gate, h, conv_w):
    seq = h.shape[1]
    u = c_gate * h
    up = jnp.pad(u, ((0, 0), (CONV_K - 1, 0), (0, 0)))
    y = conv_w[0] * up[:, 0:seq]
    for j in range(1, CONV_K):
        y = y + conv_w[j] * up[:, j:j + seq]
    return b_gate * y


def cross_attention(xn, memn, w_q, w_kv, w_o):
    bsz, seq = xn.shape[0], xn.shape[1]
    q = (xn @ w_q).reshape(bsz, seq, X_HEADS, X_HEAD_DIM)
    kv = memn @ w_kv
    k = kv[..., :D_MODEL].reshape(bsz, MEM_LEN, X_HEADS, X_HEAD_DIM)
    v = kv[..., D_MODEL:].reshape(bsz, MEM_LEN, X_HEADS, X_HEAD_DIM)
    s = jnp.einsum('bshd,bmhd->bhsm', q, k).astype(jnp.float32) * (X_HEAD_DIM ** -0.5)
    p = jax.nn.softmax(s, axis=-1)
    o = jnp.einsum('bhsm,bmhd->bshd', p.astype(v.dtype), v).reshape(bsz, seq, D_MODEL)
    return o @ w_o


def setup_inputs(seed: int = 0) -> dict:
    key = jax.random.key(seed)
    ks = jax.random.split(key, 24)
    f32 = jnp.float32

    def w(k, shape, fan_in):
        return jax.random.normal(k, shape, f32) * (fan_in ** -0.5)

    def gain(k, shape):
        return 1.0 + 0.02 * jax.random.normal(k, shape, f32)

    x = jax.random.normal(ks[0], (BATCH, SEQ, D_MODEL), f32)
    mem = jax.random.normal(ks[1], (BATCH, MEM_LEN, D_MODEL), f32)
    offset = jax.random.randint(ks[2], (BATCH, 1), 0, 1024, dtype=jnp.int32)
    positions = (offset + jnp.arange(SEQ, dtype=jnp.int32)[None, :]).astype(jnp.int32)
    return {
        "x": x,
        "mem": mem,
        "positions": positions,
        "norm_mix_g": gain(ks[3], (DEPTH, D_MODEL)),
        "w_in": w(ks[4], (DEPTH, D_MODEL, IN_COLS), D_MODEL),
        "lam_q1": 0.1 * jax.random.normal(ks[5], (DEPTH, DIFF_HEAD_DIM), f32),
        "lam_k1": 0.1 * jax.random.normal(ks[6], (DEPTH, DIFF_HEAD_DIM), f32),
        "lam_q2": 0.1 * jax.random.normal(ks[7], (DEPTH, DIFF_HEAD_DIM), f32),
        "lam_k2": 0.1 * jax.random.normal(ks[8], (DEPTH, DIFF_HEAD_DIM), f32),
        "subln_g": gain(ks[9], (DEPTH, 2 * DIFF_HEAD_DIM)),
        "conv_w": w(ks[10], (DEPTH, CONV_K, CONV_WIDTH_CH), CONV_K),
        "w_mix_out": w(ks[11], (DEPTH, D_MODEL, D_MODEL), D_MODEL),
        "norm_x_g": gain(ks[12], (DEPTH, D_MODEL)),
        "mem_norm_g": gain(ks[13], (D_MODEL,)),
        "w_xq": w(ks[14], (DEPTH, D_MODEL, D_MODEL), D_MODEL),
        "w_xkv": w(ks[15], (DEPTH, D_MODEL, 2 * D_MODEL), D_MODEL),
        "w_xo": w(ks[16], (DEPTH, D_MODEL, D_MODEL), D_MODEL),
        "norm_ffn_g": gain(ks[17], (DEPTH, D_MODEL)),
        "w_ff1": w(ks[18], (DEPTH, D_MODEL, D_FF), D_MODEL),
        "w_ff2": w(ks[19], (DEPTH, D_FF, D_MODEL), D_FF),
        "final_g": gain(ks[20], (D_MODEL,)),
    }


def reference(x, mem, positions, norm_mix_g, w_in, lam_q1, lam_k1, lam_q2, lam_k2, subln_g,
              conv_w, w_mix_out, norm_x_g, mem_norm_g, w_xq, w_xkv, w_xo, norm_ffn_g,
              w_ff1, w_ff2, final_g):
    bsz, seq = x.shape[0], x.shape[1]
    cos, sin = rotary_tables(positions)
    memn = rms_norm(mem, mem_norm_g)
    a0, a1, a2, a3 = ATTN_WIDTH, 2 * ATTN_WIDTH, 3 * ATTN_WIDTH, 3 * ATTN_WIDTH + CONV_WIDTH_CH
    a4 = a3 + CONV_WIDTH_CH
    h = x
    for l in range(DEPTH):
        lambda_init = 0.8 - 0.6 * math.exp(-0.3 * l)
        xn = rms_norm(h, norm_mix_g[l])
        proj = xn @ w_in[l]
        q = proj[..., :a0].reshape(bsz, seq, DIFF_HEADS, 2, DIFF_HEAD_DIM)
        k = proj[..., a0:a1].reshape(bsz, seq, DIFF_HEADS, 2, DIFF_HEAD_DIM)
        v = proj[..., a1:a2].reshape(bsz, seq, DIFF_HEADS, 2 * DIFF_HEAD_DIM)
        b_gate = proj[..., a2:a3]
        c_gate = proj[..., a3:a4]
        hc = proj[..., a4:]
        q = apply_partial_rotary(q, cos, sin)
        k = apply_partial_rotary(k, cos, sin)
        lam = (jnp.exp(jnp.sum(lam_q1[l].astype(jnp.float32) * lam_k1[l].astype(jnp.float32)))
               - jnp.exp(jnp.sum(lam_q2[l].astype(jnp.float32) * lam_k2[l].astype(jnp.float32)))
               + lambda_init)
        attn_out = diff_attention(q, k, v, lam, subln_g[l], lambda_init)
        conv_out = short_gated_conv(b_gate, c_gate, hc, conv_w[l])
        mixed = jnp.concatenate([attn_out, conv_out.astype(attn_out.dtype)], axis=-1)
        h = h + mixed @ w_mix_out[l]
        h = h + cross_attention(rms_norm(h, norm_x_g[l]), memn, w_xq[l], w_xkv[l], w_xo[l])
        f = rms_norm(h, norm_ffn_g[l]) @ w_ff1[l]
        f = jnp.square(jax.nn.relu(f))
        h = h + f @ w_ff2[l]
    return rms_norm(h, final_g)
```

```python
import math
from contextlib import ExitStack

import numpy as np
import concourse.bass as bass
import concourse.mybir as mybir
from concourse.bass_utils import run_bass_kernel_spmd

F32 = mybir.dt.float32
BF16 = mybir.dt.bfloat16
I32 = mybir.dt.int32
AF = mybir.ActivationFunctionType
ALU = mybir.AluOpType
AX = mybir.AxisListType

COMPUTE = ("pe", "act", "dve", "pool")


class _Rec:
    def __getattr__(self, name):
        def f(*a, **k):
            self.call = (name, a, k)
            return self
        return f


class Sched:
    def __init__(self, nc, stack):
        self.nc = nc
        self.stack = stack
        self.ops = []
        self.lw = {}
        self.rd = {}
        self.eng_obj = {"pe": nc.tensor, "act": nc.scalar, "dve": nc.vector,
                        "pool": nc.gpsimd, "sp": nc.sync}
        self.stream_ops = {}

    def add(self, eng, fn, reads=(), writes=(), stream=None):
        rec = _Rec()
        fn(rec)
        name, args, kw = rec.call
        idx = len(self.ops)
        deps = {}
        reads = list(reads)
        writes = list(writes)
        for k in reads:
            w = self.lw.get(k)
            if w is not None:
                deps[w] = "raw"
        for k in writes:
            w = self.lw.get(k)
            if w is not None and w not in deps:
                deps[w] = "waw"
            for r in self.rd.get(k, {}).values():
                if r not in deps:
                    deps[r] = "war"
        wset = set(writes)
        for k in writes:
            self.lw[k] = idx
            self.rd[k] = {}
        for k in reads:
            if k in wset:
                continue
            d = self.rd.setdefault(k, {})
            if stream is not None:
                d[("dma", idx)] = idx
            else:
                d[eng] = idx
        deps.pop(idx, None)
        self.ops.append(dict(eng=eng, name=name, args=args, kw=kw, deps=deps, stream=stream))
        if stream is not None:
            self.stream_ops.setdefault(stream, []).append(idx)
        return idx

    def _skip(self, op, dop, kind):
        if dop["eng"] == op["eng"] and op["stream"] is None and dop["stream"] is None:
            if op["eng"] == "pe":
                return True
        return False

    def emit(self, final_streams=()):
        nc = self.nc
        ops = self.ops
        signal = [False] * len(ops)
        for i, op in enumerate(ops):
            for d, kind in op["deps"].items():
                dop = ops[d]
                if dop["stream"] is not None:
                    continue
                if self._skip(op, dop, kind):
                    continue
                signal[d] = True
        esem = {e: self.stack.enter_context(nc.semaphore("sem_" + e)) for e in COMPUTE}
        ssem = {}
        for s in self.stream_ops:
            ssem[s] = self.stack.enter_context(nc.semaphore("dsem_%d" % len(ssem)))
        cnt = {e: 0 for e in COMPUTE}
        val = [0] * len(ops)
        scnt = {s: 0 for s in self.stream_ops}
        waited = {e: {} for e in self.eng_obj}
        nwait = 0
        for i, op in enumerate(ops):
            eng = op["eng"]
            eo = self.eng_obj[eng]
            need = {}
            for d, kind in op["deps"].items():
                dop = ops[d]
                if dop["stream"] is not None:
                    s = dop["stream"]
                    key = ("s", s)
                    v = 16 * scnt[s]
                else:
                    if self._skip(op, dop, kind):
                        continue
                    key = ("e", dop["eng"])
                    v = val[d]
                if v > need.get(key, 0):
                    need[key] = v
            for key, v in need.items():
                if waited[eng].get(key, 0) >= v:
                    continue
                waited[eng][key] = v
                sem = ssem[key[1]] if key[0] == "s" else esem[key[1]]
                eo.wait_ge(sem, v)
                nwait += 1
            ins = getattr(eo, op["name"])(*op["args"], **op["kw"])
            if op["stream"] is not None:
                s = op["stream"]
                scnt[s] += 1
                ins.then_inc(ssem[s], 16)
            elif signal[i]:
                cnt[eng] += 1
                val[i] = cnt[eng]
                ins.then_inc(esem[eng], 1)
            else:
                val[i] = cnt[eng]
        for s in final_streams:
            self.eng_obj["sp"].wait_ge(ssem[s], 16 * scnt[s])
        self.stats = dict(n_ops=len(ops), n_wait=nwait, cnt=cnt, n_streams=len(ssem),
                          max_stream=max(scnt.values()) if scnt else 0)
        return self.stats


D = 1024
SEQ = 2048
MEM = 256
DEPTH = 2
NSEQ = 2
KC = 8
TC = 512
NTC = SEQ // TC
NCH = 128
NSLOT = 4
SCALE_A = 64 ** -0.5
SCALE_X = 256 ** -0.5
LAMBDA_INIT = [0.8 - 0.6 * math.exp(-0.3 * l) for l in range(DEPTH)]

C_GMIX, C_GX, C_GFFN, C_GFIN, C_GMEM, C_CONVW, C_INVF, C_GSUB, C_INVFLO = 0, 16, 32, 48, 56, 64, 88, 89, 91
C_IDENT, C_MASK, C_PERM, NCONST = 96, 224, 352, 480
LAMG = 4 * 64 + 128


def build_program(n_seq=NSEQ, depth=DEPTH, stop_phase=99, final_norm=True):
    nc = bass.Bass("TRN2", target_bir_lowering=False)
    xT = nc.dram_tensor("xT", [NSEQ, KC, 128, SEQ], F32, kind="ExternalInput").ap()
    memT = nc.dram_tensor("memT", [NSEQ, 128, KC * MEM], F32, kind="ExternalInput").ap()
    pos = nc.dram_tensor("pos", [NSEQ, SEQ], I32, kind="ExternalInput").ap()
    wst = nc.dram_tensor("wst", [DEPTH * NCH, 128, KC, 128], F32, kind="ExternalInput").ap()
    consts = nc.dram_tensor("consts", [128, NCONST], F32, kind="ExternalInput").ap()
    lamg = nc.dram_tensor("lamg", [1, DEPTH * LAMG], F32, kind="ExternalInput").ap()
    outT = nc.dram_tensor("outT", [NSEQ, KC, 128, SEQ], F32, kind="ExternalOutput").ap()

    with ExitStack() as st:
        S = Sched(nc, st)
        sb = lambda n, s, d: st.enter_context(nc.sbuf_tensor(n, s, d))
        hT = sb("hT", [128, KC, SEQ], F32)
        bX = sb("bX", [128, KC, SEQ], BF16)
        bA = sb("bA", [128, KC, SEQ], BF16)
        bC = sb("bC", [128, 4, SEQ], BF16)
        bV = sb("bV", [128, 16, 516], BF16)
        ropeC = sb("ropeC", [128, SEQ], BF16)
        ropeS = sb("ropeS", [128, SEQ], BF16)
        tI = sb("tI", [128, TC], I32)
        E = [sb("E%d" % i, [128, 2 * TC], BF16) for i in range(2)]
        U = sb("U", [128, SEQ + 2], F32)
        tF = [sb("tF%d" % i, [128, TC], F32) for i in range(4)]
        wb = [sb("wb%d" % i, [128, KC, 128], BF16) for i in range(NSLOT)]
        cst = sb("cst", [128, C_IDENT], F32)
        identb = sb("identb", [128, 128], BF16)
        maskneg = sb("maskneg", [128, 128], BF16)
        permb = sb("permb", [128, 128], BF16)
        onesD = sb("onesD", [128, 128], BF16)
        ones1 = sb("ones1", [128, 128], BF16)
        eps_n = sb("eps_n", [128, 1], F32)
        eps_s = sb("eps_s", [128, 1], F32)
        neglam = sb("neglam", [128, DEPTH], F32)
        gsT = sb("gsT", [128, DEPTH], F32)
        lamt = sb("lamt", [128, 64], F32)
        lams = sb("lams", [128, 4], F32)
        Ost = sb("Ost", [128, 1032], F32)
        zr = sb("zr", [128, 8], F32)
        ssb = sb("ssb", [128, 12], F32)
        zr2 = sb("zr2", [128, 8], F32)
        ssb2 = sb("ssb2", [128, 12], F32)
        pconst = sb("pconst", [128, 12], F32)
        obw = sb("obw", [128, 512], BF16)
        sqt = sb("sqt", [128, 512], F32)
        pp = [st.enter_context(nc.psum_tensor("pp%d" % i, [128, 1024], F32)) for i in range(4)]
        ps = [pp[b // 2][:, (b % 2) * 512:(b % 2) * 512 + 512] for b in range(8)]
        ps7b = pp[3].bitcast(BF16)[:, 1024:2048]

        def tk(name, kc, c0, c1):
            return [(name, kc, j) for j in range(c0 // 128, (c1 + 127) // 128)]

        def uk(c0, c1):
            return [("U", j) for j in range(c0 // 128, (c1 + 127) // 128)]

        UALL = uk(0, SEQ + 2)
        PS = lambda b: ("ps", b)
        rr = dict(tF=0, bank=0, E=0, rq=0)

        def nxt(name, n):
            v = rr[name]
            rr[name] = (v + 1) % n
            return v

        gbs = dict(n=8, i=0)

        def gbank():
            gbs["i"] = (gbs["i"] + 1) % gbs["n"]
            return gbs["i"]

        wstate = dict(issued=0, used=0)
        twopass = stop_phase >= 99

        def layer_order(l):
            o = list(range(0, 24))
            mix = list(range(24, 32))
            o += mix + (mix if twopass else [])
            o += list(range(32, 56))
            xo = list(range(56, 64))
            o += xo + (xo if twopass else [])
            for b in range(4):
                o += list(range(64 + 16 * b, 72 + 16 * b))
                f2 = list(range(72 + 16 * b, 80 + 16 * b))
                o += f2 + (f2 if (twopass and b == 3) else [])
            return [l * NCH + i for i in o]

        worder = []
        for s_ in range(n_seq):
            for l_ in range(depth):
                worder += layer_order(l_)
        total_chunks = len(worder)

        def w_issue_upto(g):
            while wstate["issued"] <= min(g, total_chunks - 1):
                gi = wstate["issued"]
                slot = gi % NSLOT
                src = wst[worder[gi]]
                S.add("pool", (lambda slot, src: lambda e: e.dma_start(out=wb[slot][:], in_=src))(slot, src),
                      writes=[("w", slot)], stream=("w", slot))
                wstate["issued"] += 1

        def w_next(prefetch=True):
            g = wstate["used"]
            wstate["used"] += 1
            w_issue_upto(g + NSLOT - 1 if prefetch else g)
            return g % NSLOT

        def w_skip(n):
            wstate["used"] += n
            wstate["issued"] = max(wstate["issued"], wstate["used"])

        S.add("sp", lambda e: e.dma_start(out=cst[:], in_=consts[:, 0:C_IDENT]), writes=["cst"], stream="cst")
        S.add("sp", lambda e: e.dma_start(out=U[:, 0:DEPTH * LAMG], in_=lamg.partition_broadcast(128)),
              writes=uk(0, DEPTH * LAMG), stream="lamg")
        MOFF = 1024 - C_IDENT
        S.add("sp", lambda e: e.dma_start(out=U[:, 1024:1024 + NCONST - C_IDENT], in_=consts[:, C_IDENT:NCONST]),
              writes=uk(1024, 1024 + NCONST - C_IDENT), stream="cmat")
        lamg_sb = U
        LK = uk(0, DEPTH * LAMG)
        CMK = uk(1024, 1024 + NCONST - C_IDENT)
        w_issue_upto(NSLOT - 2)
        S.add("dve", lambda e: e.tensor_copy(out=identb[:], in_=U[:, MOFF + C_IDENT:MOFF + C_IDENT + 128]),
              reads=CMK, writes=["identb"])
        S.add("dve", lambda e: e.tensor_scalar(out=maskneg[:], in0=U[:, MOFF + C_MASK:MOFF + C_MASK + 128], scalar1=-1.0, scalar2=30000.0,
                                               op0=ALU.add, op1=ALU.mult), reads=CMK, writes=["maskneg"])
        S.add("dve", lambda e: e.tensor_copy(out=permb[:], in_=U[:, MOFF + C_PERM:MOFF + C_PERM + 128]),
              reads=CMK, writes=["permb"])
        S.add("dve", lambda e: e.memset(onesD[:], 1.0 / D), writes=["onesD"])
        S.add("dve", lambda e: e.memset(ones1[:], 1.0), writes=["ones1"])
        S.add("dve", lambda e: e.memset(eps_n[:], 1e-6), writes=["eps_n"])
        S.add("dve", lambda e: e.memset(eps_s[:], 1e-5), writes=["eps_s"])
        S.add("dve", lambda e: e.memset(pconst[:, 0:4], 1.0 / 128), writes=[("pconst", 0)])
        S.add("dve", lambda e: e.memset(pconst[:, 4:8], 1e-5), writes=[("pconst", 1)])
        S.add("dve", lambda e: e.memset(pconst[:, 8:12], -0.5), writes=[("pconst", 2)])
        S.add("dve", lambda e: e.memset(bV[:], 1.0), writes=[("bV", t) for t in range(16)])
        for l in range(depth):
            base = l * LAMG
            for i in range(2):
                S.add("dve", (lambda base, i: lambda e: e.tensor_tensor(
                    out=lamt[:], in0=lamg_sb[:, base + 128 * i: base + 128 * i + 64],
                    in1=lamg_sb[:, base + 128 * i + 64: base + 128 * i + 128], op=ALU.mult))(base, i),
                    reads=LK, writes=["lamt"])
                S.add("dve", (lambda i: lambda e: e.tensor_reduce(out=lams[:, i:i + 1], in_=lamt[:], axis=AX.X, op=ALU.add))(i),
                      reads=["lamt"], writes=[("lams", i)])
                S.add("act", (lambda i: lambda e: e.activation(out=lams[:, 2 + i:3 + i], in_=lams[:, i:i + 1], func=AF.Exp))(i),
                      reads=[("lams", i)], writes=[("lams", 2 + i)])
            S.add("dve", (lambda l: lambda e: e.tensor_tensor(out=neglam[:, l:l + 1], in0=lams[:, 3:4], in1=lams[:, 2:3],
                                                            op=ALU.subtract))(l),
                  reads=[("lams", 2), ("lams", 3)], writes=[("neglam", l)])
            S.add("dve", (lambda l: lambda e: e.tensor_scalar_add(out=neglam[:, l:l + 1], in0=neglam[:, l:l + 1],
                                                                scalar1=-LAMBDA_INIT[l]))(l),
                  reads=[("neglam", l)], writes=[("neglam", l)])
            S.add("dve", lambda e: e.tensor_scalar_mul(out=gsT[:, l:l + 1], in0=cst[:, C_GSUB + l:C_GSUB + l + 1], scalar1=1.0 - LAMBDA_INIT[l]),
                  reads=["cst"], writes=[("gsT", l)])

        def act_ap(buf, k, c0, cn):
            return buf[:, k, c0:c0 + cn]

        def gemm(nchunks, act_of_k, act_keys_of_k, tcols, evac, tcn=TC, t_lo=0, ci0=0):
            for ci in range(ci0, ci0 + nchunks):
                slot = w_next()
                for t0 in range(t_lo, tcols, tcn):
                    bank = gbank()
                    for k in range(KC):
                        S.add("pe", (lambda bank, slot, k, t0: lambda e: e.matmul(
                            ps[bank][:, 0:tcn], lhsT=wb[slot][:, k, :], rhs=act_of_k(k, t0, tcn),
                            start=(k == 0), stop=(k == KC - 1)))(bank, slot, k, t0),
                            reads=[("w", slot)] + act_keys_of_k(k, t0, tcn), writes=[PS(bank)])
                    evac(ci, t0, bank)

        late_norm = []

        def flush_late():
            while late_norm:
                late_norm.pop(0)()

        def gemm_multi(evacs, act_of_k, act_keys_of_k, halves, ci0=0):
            slots = [w_next(prefetch=(i == 0)) for i in range(len(evacs))]
            for hi_, (lo, hi) in enumerate(halves):
                if hi_ == 1:
                    flush_late()
                for t0 in range(lo, hi, TC):
                    for i, ev in enumerate(evacs):
                        bank = gbank()
                        slot = slots[i]
                        for k in range(KC):
                            S.add("pe", lambda e: e.matmul(ps[bank][:, 0:TC], lhsT=wb[slot][:, k, :], rhs=act_of_k(k, t0, TC),
                                                           start=(k == 0), stop=(k == KC - 1)),
                                  reads=[("w", slot)] + act_keys_of_k(k, t0, TC), writes=[PS(bank)])
                        ev(ci0 + i, t0, bank)

        def rms_norm(src_of_k, src_keys_of_k, gcol, dst_of_k, dst_keys_of_k, ncols, cn, src_wide=None, t_lo=0, part="all"):
            if part == "sq":
                for t0 in range(t_lo, ncols, cn):
                    for k in range(KC):
                        S.add("act", lambda e: e.activation(out=bA[:, k, t0:t0 + cn], in_=src_of_k(k, t0, cn), func=AF.Square),
                              reads=src_keys_of_k(k, t0, cn), writes=tk("bA", k, t0, t0 + cn))
                return
            for t0 in range(t_lo, ncols, cn):
                bank = gbank()
                if src_wide is not None:
                    S.add("act", lambda e: e.activation(out=bA[:, :, t0:t0 + cn], in_=src_wide(t0, cn), func=AF.Square),
                          reads=[k_ for k in range(KC) for k_ in src_keys_of_k(k, t0, cn)],
                          writes=[k_ for k in range(KC) for k_ in tk("bA", k, t0, t0 + cn)])
                for k in range(KC):
                    if src_wide is None and part == "all":
                        S.add("act", lambda e: e.activation(out=bA[:, k, t0:t0 + cn], in_=src_of_k(k, t0, cn), func=AF.Square),
                              reads=src_keys_of_k(k, t0, cn), writes=tk("bA", k, t0, t0 + cn))
                    S.add("pe", lambda e: e.matmul(ps[bank][:, 0:cn], lhsT=onesD[:], rhs=bA[:, k, t0:t0 + cn],
                                                   start=(k == 0), stop=(k == KC - 1)),
                          reads=tk("bA", k, t0, t0 + cn) + ["onesD"], writes=[PS(bank)])
                f = nxt("tF", 4)
                S.add("act", lambda e: e.activation(out=tF[f][:, 0:cn], in_=ps[bank][:, 0:cn], func=AF.Ln, bias=eps_n[:, 0:1], scale=1.0),
                      reads=["eps_n"], writes=[PS(bank), ("tF", f)])
                S.add("act", lambda e: e.activation(out=tF[f][:, 0:cn], in_=tF[f][:, 0:cn], func=AF.Exp, scale=-0.5),
                      reads=[("tF", f)], writes=[("tF", f)])
                for k in range(KC):
                    S.add("dve", (lambda f, k, t0: lambda e: e.scalar_tensor_tensor(
                        out=dst_of_k(k, t0, cn), in0=src_of_k(k, t0, cn), scalar=cst[:, gcol + k:gcol + k + 1],
                        in1=tF[f][:, 0:cn], op0=ALU.mult, op1=ALU.mult))(f, k, t0),
                        reads=src_keys_of_k(k, t0, cn) + [("tF", f), "cst"], writes=dst_keys_of_k(k, t0, cn))

        h_wide = lambda t0, cn: hT[:, :, t0:t0 + cn]
        h_of = lambda k, t0, cn: hT[:, k, t0:t0 + cn]
        h_keys = lambda k, t0, cn: tk("h", k, t0, t0 + cn)
        bX_of = lambda k, t0, cn: bX[:, k, t0:t0 + cn]
        bX_keys = lambda k, t0, cn: tk("bX", k, t0, t0 + cn)
        bA_of = lambda k, t0, cn: bA[:, k, t0:t0 + cn]
        bA_keys = lambda k, t0, cn: tk("bA", k, t0, t0 + cn)

        def resid_evac(ci, t0, bank):
            S.add("dve", lambda e: e.tensor_tensor(out=hT[:, ci, t0:t0 + TC], in0=ps[bank][:, 0:TC], in1=hT[:, ci, t0:t0 + TC], op=ALU.add),
                  reads=tk("h", ci, t0, t0 + TC), writes=[PS(bank)] + tk("h", ci, t0, t0 + TC))

        def rope_setup(s, t0, use_tF=False):
            if use_tF:
                T = [Ost[:, 0:TC], Ost[:, 516:516 + TC], sqt[:, :], sqt[:, :]]
                TK = [[("Ost", n_) for n_ in range(4)], [("Ost", n_) for n_ in range(4, 8)], ["sqt"], ["sqt"]]
            else:
                T = [U[:, i * TC:(i + 1) * TC] for i in range(4)]
                TK = [uk(i * TC, (i + 1) * TC) for i in range(4)]
            IK = ["tI"]
            S.add("sp", lambda e: e.dma_start(out=tI[:], in_=pos[s:s + 1, t0:t0 + TC].partition_broadcast(128)), writes=IK, stream="pos")
            S.add("dve", lambda e: e.tensor_copy(out=T[0], in_=tI[:]), reads=IK, writes=TK[0])
            S.add("dve", lambda e: e.tensor_scalar_mul(out=T[1], in0=T[0], scalar1=cst[:, C_INVF:C_INVF + 1]), reads=TK[0] + ["cst"], writes=TK[1])
            S.add("dve", lambda e: e.scalar_tensor_tensor(out=T[1], in0=T[0], scalar=cst[:, C_INVFLO:C_INVFLO + 1], in1=T[1],
                                                          op0=ALU.mult, op1=ALU.add), reads=TK[0] + TK[1] + ["cst"], writes=TK[1])
            S.add("dve", lambda e: e.tensor_scalar_mul(out=T[1], in0=T[1], scalar1=float(1.0 / (2 * np.pi))), reads=TK[1], writes=TK[1])
            for (phase, tab, tname, fu, fk) in ((0.0, ropeS, "ropeS", 2, 0), (0.25, ropeC, "ropeC", 3, 0)):
                S.add("dve", lambda e: e.tensor_scalar_add(out=T[fu], in0=T[1], scalar1=phase), reads=TK[1], writes=TK[fu])
                S.add("dve", lambda e: e.tensor_copy(out=tI[:], in_=T[fu]), reads=TK[fu], writes=IK)
                S.add("dve", lambda e: e.tensor_copy(out=T[fk], in_=tI[:]), reads=IK, writes=TK[fk])
                S.add("dve", lambda e: e.tensor_tensor(out=T[fu], in0=T[fu], in1=T[fk], op=ALU.subtract), reads=TK[fu] + TK[fk], writes=TK[fu])
                S.add("dve", lambda e: e.tensor_single_scalar(out=T[fk], in_=T[fu], scalar=0.5, op=ALU.is_gt), reads=TK[fu], writes=TK[fk])
                S.add("dve", lambda e: e.tensor_tensor(out=T[fu], in0=T[fu], in1=T[fk], op=ALU.subtract), reads=TK[fu] + TK[fk], writes=TK[fu])
                S.add("act", lambda e: e.activation(out=tab[:, t0:t0 + TC], in_=T[fu], func=AF.Sin, scale=float(2 * np.pi)),
                      reads=TK[fu], writes=[(tname, t0 // TC)])

        def resid_gemm(act_of, act_keys, norm_fn, inject=None):
            if not twopass or norm_fn is None:
                gemm(8, act_of, act_keys, SEQ, resid_evac)
                if twopass:
                    raise RuntimeError("weight order expects two passes")
                return False
            H = SEQ // 2
            gemm(2, act_of, act_keys, H, resid_evac)
            if inject is not None:
                inject()
            gemm(6, act_of, act_keys, H, resid_evac, ci0=2)
            norm_fn(0, H, "sq")
            gemm(2, act_of, act_keys, SEQ, resid_evac, t_lo=H)
            norm_fn(0, H, "rest")
            gemm(6, act_of, act_keys, SEQ, resid_evac, t_lo=H, ci0=2)
            norm_fn(H, SEQ, "sq")
            late_norm.append(lambda: norm_fn(H, SEQ, "rest"))
            return True

        def final_norm_range(s, t_lo, t_hi, part="all"):
            if part == "sq":
                for t0 in range(t_lo, t_hi, TC):
                    for k in range(KC):
                        S.add("act", lambda e: e.activation(out=bA[:, k, t0:t0 + TC], in_=hT[:, k, t0:t0 + TC], func=AF.Square),
                              reads=tk("h", k, t0, t0 + TC), writes=tk("bA", k, t0, t0 + TC))
                return
            for t0 in range(t_lo, t_hi, TC):
                bank = gbank()
                for k in range(KC):
                    if part == "all":
                        S.add("act", lambda e: e.activation(out=bA[:, k, t0:t0 + TC], in_=hT[:, k, t0:t0 + TC], func=AF.Square),
                              reads=tk("h", k, t0, t0 + TC), writes=tk("bA", k, t0, t0 + TC))
                    S.add("pe", lambda e: e.matmul(ps[bank][:, 0:TC], lhsT=onesD[:], rhs=bA[:, k, t0:t0 + TC],
                                                   start=(k == 0), stop=(k == KC - 1)),
                          reads=tk("bA", k, t0, t0 + TC) + ["onesD"], writes=[PS(bank)])
                f = nxt("tF", 4)
                S.add("act", lambda e: e.activation(out=tF[f][:], in_=ps[bank][:, 0:TC], func=AF.Ln, bias=eps_n[:, 0:1], scale=1.0),
                      reads=["eps_n"], writes=[PS(bank), ("tF", f)])
                S.add("act", lambda e: e.activation(out=tF[f][:], in_=tF[f][:], func=AF.Exp, scale=-0.5),
                      reads=[("tF", f)], writes=[("tF", f)])
                for k in range(KC):
                    S.add("dve", lambda e: e.scalar_tensor_tensor(
                        out=hT[:, k, t0:t0 + TC], in0=hT[:, k, t0:t0 + TC], scalar=cst[:, C_GFIN + k:C_GFIN + k + 1],
                        in1=tF[f][:], op0=ALU.mult, op1=ALU.mult),
                        reads=tk("h", k, t0, t0 + TC) + [("tF", f), "cst"], writes=tk("h", k, t0, t0 + TC))
                    S.add("sp", lambda e: e.dma_start(out=outT[s, k, :, t0:t0 + TC], in_=hT[:, k, t0:t0 + TC]),
                          reads=tk("h", k, t0, t0 + TC), stream=("hout", k, t0))

        out_streams = []
        for s in range(n_seq):
            for t0 in range(0, SEQ, TC):
                for k in range(KC):
                    S.add("sp", lambda e: e.dma_start(out=hT[:, k, t0:t0 + TC], in_=xT[s, k, :, t0:t0 + TC]),
                          writes=tk("h", k, t0, t0 + TC), stream=("hin", k, t0))

            p1_done = False
            fin_done = False
            for l in range(depth):
                ph = 10 * l
                if not p1_done:
                    rms_norm(h_of, h_keys, C_GMIX + 8 * l, bX_of, bX_keys, SEQ, TC)
                p1_done = False
                if stop_phase <= ph + 1:
                    w_skip(NCH)
                    continue

                if not twopass:
                    flush_late()
                S.add("dve", lambda e: e.memset(U[:, 0:2], 0.0), writes=uk(0, 2))
                for cc in range(4):
                    def ev_h(ci, t0, bank):
                        S.add("act", lambda e: e.activation(out=U[:, 2 + t0:2 + t0 + TC], in_=ps[bank][:, 0:TC], func=AF.Copy),
                              writes=[PS(bank)] + uk(2 + t0, 2 + t0 + TC))

                    def ev_c(ci, t0, bank):
                        S.add("dve", lambda e: e.tensor_tensor(out=U[:, 2 + t0:2 + t0 + TC], in0=ps[bank][:, 0:TC],
                                                               in1=U[:, 2 + t0:2 + t0 + TC], op=ALU.mult),
                              reads=uk(2 + t0, 2 + t0 + TC), writes=[PS(bank)] + uk(2 + t0, 2 + t0 + TC))

                    def ev_b(ci, t0, bank, cc=cc):
                        wc = C_CONVW + (l * 4 + cc) * 3
                        f = nxt("tF", 4)
                        S.add("act", lambda e: e.activation(out=tF[f][:], in_=U[:, 2 + t0:2 + t0 + TC], func=AF.Copy,
                                                            scale=cst[:, wc + 2:wc + 3]),
                              reads=uk(2 + t0, 2 + t0 + TC) + ["cst"], writes=[("tF", f)])
                        S.add("dve", lambda e: e.scalar_tensor_tensor(out=tF[f][:], in0=U[:, 1 + t0:1 + t0 + TC], scalar=cst[:, wc + 1:wc + 2],
                                                                      in1=tF[f][:], op0=ALU.mult, op1=ALU.add),
                              reads=uk(1 + t0, 1 + t0 + TC) + ["cst", ("tF", f)], writes=[("tF", f)])
                        S.add("dve", lambda e: e.scalar_tensor_tensor(out=tF[f][:], in0=U[:, t0:t0 + TC], scalar=cst[:, wc:wc + 1],
                                                                      in1=tF[f][:], op0=ALU.mult, op1=ALU.add),
                              reads=uk(t0, t0 + TC) + ["cst", ("tF", f)], writes=[("tF", f)])
                        S.add("dve", lambda e: e.tensor_tensor(out=bC[:, cc, t0:t0 + TC], in0=ps[bank][:, 0:TC], in1=tF[f][:], op=ALU.mult),
                              reads=[("tF", f)], writes=[PS(bank)] + tk("bC", cc, t0, t0 + TC))

                    if twopass and cc == 0:
                        gemm_multi([ev_h, ev_c, ev_b], bX_of, bX_keys, [(0, SEQ // 2), (SEQ // 2, SEQ)])
                    else:
                        gemm(1, bX_of, bX_keys, SEQ, ev_h)
                        gemm(1, bX_of, bX_keys, SEQ, ev_c)
                        gemm(1, bX_of, bX_keys, SEQ, ev_b)
                    if s == 0 and l == 0:
                        rope_setup(0, cc * TC, use_tF=True)

                for hh in range(4):
                    slot = w_next()
                    for g in range(4):
                        bank = gbank()
                        for j in range(4):
                            tt = 4 * g + j
                            for k in range(KC):
                                S.add("pe", (lambda bank, slot, k, j, tt: lambda e: e.matmul(
                                    ps[bank][:, j * 128:(j + 1) * 128], lhsT=bX[:, k, tt * 128:(tt + 1) * 128], rhs=wb[slot][:, k, :],
                                    start=(k == 0), stop=(k == KC - 1)))(bank, slot, k, j, tt),
                                    reads=[("w", slot), ("bX", k, tt)], writes=[PS(bank)])
                        S.add("act", (lambda bank, g, hh: lambda e: e.activation(
                            out=bV[:, 4 * g:4 * g + 4, hh * 129:hh * 129 + 128],
                            in_=ps[bank][:, :].rearrange("p (j c) -> p j c", j=4), func=AF.Copy))(bank, g, hh),
                            writes=[PS(bank)] + [("bV", 4 * g + j) for j in range(4)])

                qk_pending = []
                gbs["n"] = 6
                for hh in range(4):
                    for which in range(2):
                        dstc = hh + 4 * which

                        def ev_qk(ci, t0, bank, dstc=dstc):
                            b = nxt("E", 2)
                            rb = 6 + nxt("rq", 2)
                            while qk_pending:
                                qk_pending.pop(0)()
                            S.add("act", lambda e: e.activation(out=E[b][:, 0:TC], in_=ps[bank][:, 0:TC], func=AF.Copy),
                                  writes=[PS(bank), ("E", b, 0)])

                            def stage_b():
                                S.add("pe", lambda e: e.matmul(ps[rb][:, 0:TC], lhsT=permb[:], rhs=E[b][:, 0:TC], start=True, stop=True),
                                      reads=[("E", b, 0), "permb"], writes=[PS(rb)])
                                S.add("dve", lambda e: e.tensor_tensor(out=bA[:, dstc, t0:t0 + TC], in0=ps[bank][:, 0:TC], in1=ropeC[:, t0:t0 + TC], op=ALU.mult),
                                      reads=[("ropeC", t0 // TC)], writes=[PS(bank)] + tk("bA", dstc, t0, t0 + TC))
                                S.add("dve", lambda e: e.tensor_tensor(out=E[b][:, TC:2 * TC], in0=ps[rb][:, 0:TC], in1=ropeS[:, t0:t0 + TC], op=ALU.mult),
                                      reads=[("ropeS", t0 // TC)], writes=[PS(rb), ("E", b, 1)])
                                S.add("dve", lambda e: e.tensor_tensor(out=bA[:, dstc, t0:t0 + TC], in0=bA[:, dstc, t0:t0 + TC], in1=E[b][:, TC:2 * TC], op=ALU.add),
                                      reads=[("E", b, 1)], writes=tk("bA", dstc, t0, t0 + TC))
                            qk_pending.append(stage_b)

                        gemm(1, bX_of, bX_keys, SEQ, ev_qk)
                while qk_pending:
                    qk_pending.pop(0)()
                gbs["n"] = 8
                if stop_phase <= ph + 2:
                    w_skip(NCH - 24)
                    continue

                steps = []
                for hh in range(4):
                    for qc in range(4):
                        J = 4 * qc + 4
                        for j in range(J):
                            steps.append((hh, qc, j, j == J - 1))
                pending = []

                def emit_qk(n):
                    hh, qc, j, last = steps[n]
                    q0 = max(TC * qc, 128 * j)
                    qn = TC * (qc + 1) - q0
                    sb2 = 2 * (n % 2)
                    diag = 128 * j >= TC * qc
                    for c in range(2):
                        S.add("pe", lambda e: e.matmul(
                            ps[sb2 + c][:, 0:qn], lhsT=bA[c * 64:(c + 1) * 64, 4 + hh, 128 * j:128 * j + 128],
                            rhs=bA[c * 64:(c + 1) * 64, hh, q0:q0 + qn], start=True, stop=not diag),
                            reads=[("bA", 4 + hh, j)] + tk("bA", hh, q0, q0 + qn), writes=[PS(sb2 + c)])
                        if diag:
                            S.add("pe", lambda e: e.matmul(ps[sb2 + c][:, 0:128], lhsT=identb[:], rhs=maskneg[:], start=False, stop=True),
                                  reads=["identb", "maskneg"], writes=[PS(sb2 + c)])

                def bc_ap(t, col0, n_mid, n_in, mid_stride, in_stride):
                    return bass.AP(t, col0, [[t.shape[1], 128], [mid_stride, n_mid], [in_stride, n_in]])

                STG = [Ost, U]
                O0w = [t[:, 0:516].rearrange("p (n c) -> p n c", c=129)[:, :, 0:128] for t in STG]
                O1w = [t[:, 516:1032].rearrange("p (n c) -> p n c", c=129)[:, :, 0:128] for t in STG]
                O3 = [t[:, 0:1032].rearrange("p (n c) -> p n c", c=129) for t in STG]
                OK0 = [[("Ost", n_) for n_ in range(4)], uk(0, 1032)]
                OK1 = [[("Ost", n_) for n_ in range(4, 8)], uk(0, 1032)]
                OKB = [[[("Ost", n_) for n_ in (0, 1, 2)], [("Ost", n_) for n_ in (3, 4, 5)], [("Ost", n_) for n_ in (6, 7)]],
                       [uk(0, 1032)] * 3]
                zrs = [zr, zr2]
                ssbs = [ssb, ssb2]

                def part1(si):
                    T = STG[si]
                    S.add("dve", lambda e: e.tensor_copy(out=T[:, 0:387], in_=ps[4][:, 0:387]), writes=[PS(4)] + OKB[si][0])
                    S.add("dve", lambda e: e.tensor_copy(out=T[:, 387:774], in_=ps[5][:, 0:387]), writes=[PS(5)] + OKB[si][1])
                    S.add("dve", lambda e: e.tensor_copy(out=T[:, 774:1032], in_=ps[6][:, 0:258]), writes=[PS(6)] + OKB[si][2])

                def part2a(hh, qc, si):
                    z_, sb_ = zrs[si], ssbs[si]
                    S.add("dve", lambda e: e.reciprocal(out=z_[:, 0:8], in_=O3[si][:, :, 128]), reads=OK0[si] + OK1[si], writes=[("zr", si)])
                    S.add("dve", lambda e: e.tensor_scalar_mul(out=z_[:, 4:8], in0=z_[:, 4:8], scalar1=neglam[:, l:l + 1]),
                          reads=[("zr", si), ("neglam", l)], writes=[("zr", si)])
                    S.add("dve", lambda e: e.tensor_tensor(out=O0w[si], in0=O0w[si], in1=bc_ap(z_, 0, 4, 128, 1, 0), op=ALU.mult),
                          reads=[("zr", si)], writes=OK0[si])
                    S.add("dve", lambda e: e.tensor_tensor(out=O1w[si], in0=O1w[si], in1=bc_ap(z_, 4, 4, 128, 1, 0), op=ALU.mult),
                          reads=[("zr", si)], writes=OK1[si])
                    S.add("dve", lambda e: e.tensor_tensor(out=O0w[si], in0=O0w[si], in1=O1w[si], op=ALU.add), reads=OK1[si], writes=OK0[si])
                    S.add("dve", lambda e: e.tensor_tensor(out=sqt[:, :].rearrange("p (n c) -> p n c", c=128), in0=O0w[si], in1=O0w[si], op=ALU.mult),
                          reads=OK0[si], writes=["sqt"])
                    S.add("dve", lambda e: e.tensor_reduce(out=sb_[:, 0:4], in_=sqt[:, :].rearrange("p (n c) -> p n c", c=128), axis=AX.X, op=ALU.add),
                          reads=["sqt"], writes=[("ssb", si)])

                def part2b(hh, qc, si):
                    sb_ = ssbs[si]
                    S.add("pool", lambda e: e.tensor_tensor(out=sb_[:, 4:8], in0=sb_[:, 0:4], in1=pconst[:, 0:4], op=ALU.mult),
                          reads=[("ssb", si), ("pconst", 0)], writes=[("ssb_ln", si)])
                    S.add("pool", lambda e: e.tensor_tensor(out=sb_[:, 4:8], in0=sb_[:, 4:8], in1=pconst[:, 4:8], op=ALU.add),
                          reads=[("ssb_ln", si), ("pconst", 1)], writes=[("ssb_ln", si)])
                    S.add("pool", lambda e: e.tensor_tensor(out=sb_[:, 8:12], in0=sb_[:, 4:8], in1=pconst[:, 8:12], op=ALU.pow),
                          reads=[("ssb_ln", si), ("pconst", 2)], writes=[("ssb_r", si)])
                    S.add("dve", lambda e: e.tensor_tensor(out=obw[:, :].rearrange("p (n c) -> p n c", c=128), in0=O0w[si],
                                                           in1=bc_ap(sb_, 8, 4, 128, 1, 0), op=ALU.mult),
                          reads=OK0[si] + [("ssb_r", si)], writes=["obw"])

                def part2c(hh, qc):
                    for a in range(4):
                        S.add("pe", lambda e: e.transpose(ps7b[:, a * 128:(a + 1) * 128], obw[:, a * 128:(a + 1) * 128], identb[:]),
                              reads=["obw", "identb"], writes=[PS(7)])
                    S.add("dve", lambda e: e.tensor_scalar_mul(out=bX[:, hh, TC * qc:TC * (qc + 1)], in0=ps7b[:, 0:TC], scalar1=gsT[:, l:l + 1]),
                          reads=[("gsT", l)], writes=[PS(7)] + tk("bX", hh, TC * qc, TC * (qc + 1)))

                def flush(upto):
                    keep = []
                    for due, fn, a_ in pending:
                        if due <= upto:
                            fn(*a_)
                        else:
                            keep.append((due, fn, a_))
                    pending[:] = keep

                emit_qk(0)
                emit_qk(1)
                first_touch = set()
                chunk_no = [0]
                for n in range(len(steps)):
                    hh, qc, j, last = steps[n]
                    if j == 0:
                        first_touch = set()
                    q0 = max(TC * qc, 128 * j)
                    qn = TC * (qc + 1) - q0
                    eb = n % 2
                    sb2 = 2 * eb
                    S.add("act", lambda e: e.activation(out=E[eb][:, :].rearrange("p (c q) -> p c q", c=2)[:, :, 0:qn],
                                                        in_=pp[eb][:, :].rearrange("p (c q) -> p c q", c=2)[:, :, 0:qn], func=AF.Exp, scale=SCALE_A),
                          writes=[PS(sb2), PS(sb2 + 1), ("E", eb, 0), ("E", eb, 1)])
                    if n + 2 < len(steps):
                        emit_qk(n + 2)
                    for i in range(max(4 * qc, j), 4 * qc + 4):
                        qi0 = 128 * i - q0
                        a = i - 4 * qc
                        for c in range(2):
                            nacc = c * 4 + a
                            ob_ = 4 + nacc // 3
                            col = (nacc % 3) * 129
                            st_ = ob_ not in first_touch
                            first_touch.add(ob_)
                            S.add("pe", lambda e: e.matmul(
                                ps[ob_][:, col:col + 129], lhsT=E[eb][:, c * TC + qi0:c * TC + qi0 + 128],
                                rhs=bV[:, j, hh * 129:(hh + 1) * 129], start=st_, stop=(j == i), skip_group_check=True),
                                reads=[("E", eb, c), ("bV", j)], writes=[PS(ob_)])
                    flush(n)
                    if last:
                        si = chunk_no[0] % 2
                        chunk_no[0] += 1
                        part1(si)
                        part2a(hh, qc, si)
                        pending.append((n + 6, part2b, (hh, qc, si)))
                        pending.append((n + 9, part2c, (hh, qc)))
                late_pe = []
                for due, fn, a_ in pending:
                    if fn is part2c and twopass:
                        late_pe.append((fn, a_))
                    else:
                        fn(*a_)
                pending[:] = []
                if stop_phase <= ph + 3:
                    w_skip(NCH - 24)
                    continue

                mix_of = lambda k, t0, cn: (bX[:, k, t0:t0 + cn] if k < 4 else bC[:, k - 4, t0:t0 + cn])
                mix_keys = lambda k, t0, cn: (tk("bX", k, t0, t0 + cn) if k < 4 else tk("bC", k - 4, t0, t0 + cn))
                n5 = resid_gemm(mix_of, mix_keys, (lambda lo, hi, part, l=l: rms_norm(h_of, h_keys, C_GX + 8 * l, bX_of, bX_keys, hi, TC, t_lo=lo, part=part)) if twopass else None,
                                inject=lambda: [fn(*a_) for fn, a_ in late_pe]) \
                    if twopass else (gemm(8, mix_of, mix_keys, SEQ, resid_evac) or False)
                if stop_phase <= ph + 4:
                    w_skip(NCH - 32)
                    continue

                flush_late()
                if not n5:
                    rms_norm(h_of, h_keys, C_GX + 8 * l, bX_of, bX_keys, SEQ, TC)
                S.add("sp", (lambda s: lambda e: e.dma_start(out=U[:, 0:KC * MEM], in_=memT[s]))(s),
                      writes=uk(0, KC * MEM), stream="mem")
                m_of = lambda k, t0, cn: U[:, k * MEM + t0:k * MEM + t0 + cn]
                m_keys = lambda k, t0, cn: uk(k * MEM + t0, k * MEM + t0 + cn)
                mn_of = lambda k, t0, cn: bC[:, 2, k * MEM + t0:k * MEM + t0 + cn]
                mn_keys = lambda k, t0, cn: tk("bC", 2, k * MEM + t0, k * MEM + t0 + cn)
                rms_norm(m_of, m_keys, C_GMEM, mn_of, mn_keys, MEM, MEM)

                for c8 in range(8):
                    def ev_xq(ci, t0, bank, c8=c8):
                        S.add("act", lambda e: e.activation(out=bA[:, c8, t0:t0 + TC], in_=ps[bank][:, 0:TC], func=AF.Copy),
                              writes=[PS(bank)] + tk("bA", c8, t0, t0 + TC))
                    gemm(1, bX_of, bX_keys, SEQ, ev_xq)

                    def ev_xk(ci, t0, bank, c8=c8):
                        S.add("act", lambda e: e.activation(out=bC[:, 0, c8 * MEM:(c8 + 1) * MEM], in_=ps[bank][:, 0:MEM], func=AF.Copy),
                              writes=[PS(bank)] + tk("bC", 0, c8 * MEM, (c8 + 1) * MEM))
                    gemm(1, mn_of, mn_keys, MEM, ev_xk, tcn=MEM)
                    slot = w_next()
                    bank = gbank()
                    for mt in range(2):
                        for k in range(KC):
                            S.add("pe", lambda e: e.matmul(
                                ps[bank][:, mt * 128:(mt + 1) * 128], lhsT=bC[:, 2, k * MEM + mt * 128:k * MEM + (mt + 1) * 128],
                                rhs=wb[slot][:, k, :], start=(k == 0), stop=(k == KC - 1)),
                                reads=[("w", slot), ("bC", 2, 2 * k + mt)], writes=[PS(bank)])
                    S.add("act", lambda e: e.activation(
                        out=bC[:, 1, :].rearrange("p (m c) -> p m c", m=2)[:, :, c8 * 128:(c8 + 1) * 128],
                        in_=ps[bank][:, 0:256].rearrange("p (m c) -> p m c", m=2), func=AF.Copy),
                        writes=[PS(bank), ("bC", 1, c8), ("bC", 1, 8 + c8)])
                xsteps = [(hh, t0) for hh in range(4) for t0 in range(0, SEQ, TC)]

                def x_qk(n):
                    hh, t0 = xsteps[n]
                    eb = n % 2
                    for mt in range(2):
                        bank = 2 * eb + mt
                        for dc in range(2):
                            S.add("pe", lambda e: e.matmul(
                                ps[bank][:, 0:TC], lhsT=bC[:, 0, (2 * hh + dc) * MEM + mt * 128:(2 * hh + dc) * MEM + (mt + 1) * 128],
                                rhs=bA[:, 2 * hh + dc, t0:t0 + TC], start=(dc == 0), stop=(dc == 1)),
                                reads=[("bC", 0, 2 * (2 * hh + dc) + mt)] + tk("bA", 2 * hh + dc, t0, t0 + TC), writes=[PS(bank)])

                x_qk(0)
                x_qk(1)
                for n in range(len(xsteps)):
                    hh, t0 = xsteps[n]
                    eb = n % 2
                    for mt in range(2):
                        bank = 2 * eb + mt
                        S.add("act", lambda e: e.activation(out=E[eb][:, mt * TC:(mt + 1) * TC], in_=ps[bank][:, 0:TC], func=AF.Exp, scale=SCALE_X),
                              writes=[PS(bank), ("E", eb, mt)])
                    for mt in range(2):
                        S.add("pe", lambda e: e.matmul(ps[6][:, 0:TC], lhsT=ones1[:], rhs=E[eb][:, mt * TC:(mt + 1) * TC],
                                                       start=(mt == 0), stop=(mt == 1)),
                              reads=["ones1", ("E", eb, mt)], writes=[PS(6)])
                    if n + 2 < len(xsteps):
                        x_qk(n + 2)
                    for dvc in range(2):
                        for mt in range(2):
                            S.add("pe", lambda e: e.matmul(
                                ps[4 + dvc][:, 0:TC], lhsT=bC[:, 1, mt * 1024 + hh * 256 + dvc * 128:mt * 1024 + hh * 256 + (dvc + 1) * 128],
                                rhs=E[eb][:, mt * TC:(mt + 1) * TC], start=(mt == 0), stop=(mt == 1)),
                                reads=[("bC", 1, mt * 8 + 2 * hh + dvc), ("E", eb, mt)], writes=[PS(4 + dvc)])
                    f = nxt("tF", 4)
                    S.add("act", lambda e: e.activation(out=tF[f][:], in_=ps[6][:, 0:TC], func=AF.Ln), writes=[PS(6), ("tF", f)])
                    S.add("act", lambda e: e.activation(out=tF[f][:], in_=tF[f][:], func=AF.Exp, scale=-1.0), reads=[("tF", f)], writes=[("tF", f)])
                    for dvc in range(2):
                        S.add("dve", lambda e: e.tensor_tensor(out=bX[:, 2 * hh + dvc, t0:t0 + TC], in0=ps[4 + dvc][:, 0:TC],
                                                               in1=tF[f][:], op=ALU.mult),
                              reads=[("tF", f)], writes=[PS(4 + dvc)] + tk("bX", 2 * hh + dvc, t0, t0 + TC))
                n7 = resid_gemm(bX_of, bX_keys, (lambda lo, hi, part, l=l: rms_norm(h_of, h_keys, C_GFFN + 8 * l, bX_of, bX_keys, hi, TC, t_lo=lo, part=part))) \
                    if twopass else (gemm(8, bX_of, bX_keys, SEQ, resid_evac) or False)
                if stop_phase <= ph + 6:
                    w_skip(NCH - 72)
                    continue

                if not n7:
                    rms_norm(h_of, h_keys, C_GFFN + 8 * l, bX_of, bX_keys, SEQ, TC)
                for blk in range(4):
                    def ev_f1(ci, t0, bank):
                        f = nxt("tF", 4)
                        S.add("act", lambda e: e.activation(out=tF[f][:], in_=ps[bank][:, 0:TC], func=AF.Relu),
                              writes=[PS(bank), ("tF", f)])
                        S.add("dve", lambda e: e.tensor_tensor(out=bA[:, ci, t0:t0 + TC], in0=tF[f][:], in1=tF[f][:], op=ALU.mult),
                              reads=[("tF", f)], writes=tk("bA", ci, t0, t0 + TC))
                    if twopass and blk == 0:
                        gemm_multi([ev_f1] * 3, bX_of, bX_keys, [(0, SEQ // 2), (SEQ // 2, SEQ)])
                        gemm(5, bX_of, bX_keys, SEQ, ev_f1, ci0=3)
                    else:
                        gemm(8, bX_of, bX_keys, SEQ, ev_f1)
                    if l == depth - 1 and s + 1 < n_seq:
                        rope_setup(s + 1, blk * TC)
                    if twopass and blk == 3:
                        if l + 1 < depth:
                            nfn = lambda lo, hi, part, l=l: rms_norm(h_of, h_keys, C_GMIX + 8 * (l + 1), bX_of, bX_keys, hi, TC, t_lo=lo, part=part)
                            p1_done = resid_gemm(bA_of, bA_keys, nfn)
                        elif final_norm:
                            fin_done = resid_gemm(bA_of, bA_keys, lambda lo, hi, part, s=s: final_norm_range(s, lo, hi, part))
                        else:
                            resid_gemm(bA_of, bA_keys, lambda lo, hi, part: None)
                    else:
                        gemm(8, bA_of, bA_keys, SEQ, resid_evac)

            flush_late()
            if final_norm and not fin_done:
                final_norm_range(s, 0, SEQ)
            if not final_norm:
                for k in range(KC):
                    S.add("sp", lambda e: e.dma_start(out=outT[s, k], in_=hT[:, k, :]), reads=tk("h", k, 0, SEQ), stream=("hout", k, 0))
        stats = S.emit(final_streams=[st_ for st_ in S.stream_ops if st_[0] == "hout"])
    return nc, stats


def _chunk(Wsub):
    return np.ascontiguousarray(Wsub.reshape(KC, 128, 128).transpose(1, 0, 2))


def prep_weights(w_in, w_mix_out, w_xq, w_xkv, w_xo, w_ff1, w_ff2):
    out = np.empty((DEPTH * NCH, 128, KC, 128), np.float32)
    g = 0
    for l in range(DEPTH):
        cols = []
        for cc in range(4):
            cols += [2560 + 128 * cc, 2048 + 128 * cc, 1536 + 128 * cc]
        for hh in range(4):
            cols.append(1024 + 128 * hh)
        for hh in range(4):
            cols += [128 * hh, 512 + 128 * hh]
        for c0 in cols:
            out[g] = _chunk(w_in[l][:, c0:c0 + 128]); g += 1
        for c in range(8):
            out[g] = _chunk(w_mix_out[l][:, c * 128:(c + 1) * 128]); g += 1
        for c in range(8):
            out[g] = _chunk(w_xq[l][:, c * 128:(c + 1) * 128]); g += 1
            out[g] = _chunk(w_xkv[l][:, c * 128:(c + 1) * 128]); g += 1
            out[g] = _chunk(w_xkv[l][:, 1024 + c * 128:1024 + (c + 1) * 128]); g += 1
        for c in range(8):
            out[g] = _chunk(w_xo[l][:, c * 128:(c + 1) * 128]); g += 1
        for b in range(4):
            for c in range(8):
                out[g] = _chunk(w_ff1[l][:, b * 1024 + c * 128:b * 1024 + (c + 1) * 128]); g += 1
            for c in range(8):
                out[g] = _chunk(w_ff2[l][b * 1024:(b + 1) * 1024, c * 128:(c + 1) * 128]); g += 1
    assert g == DEPTH * NCH
    return out


def prep_consts(norm_mix_g, norm_x_g, norm_ffn_g, final_g, mem_norm_g, conv_w, subln_g):
    c = np.zeros((128, NCONST), np.float32)
    pk = lambda v: np.asarray(v, np.float32).reshape(KC, 128).T
    for l in range(DEPTH):
        c[:, C_GMIX + 8 * l:C_GMIX + 8 * l + 8] = pk(norm_mix_g[l])
        c[:, C_GX + 8 * l:C_GX + 8 * l + 8] = pk(norm_x_g[l])
        c[:, C_GFFN + 8 * l:C_GFFN + 8 * l + 8] = pk(norm_ffn_g[l])
        for cc in range(4):
            for j in range(3):
                c[:, C_CONVW + (l * 4 + cc) * 3 + j] = conv_w[l][j, cc * 128:(cc + 1) * 128]
    for l in range(DEPTH):
        c[:, C_GSUB + l] = subln_g[l]
    c[:, C_GFIN:C_GFIN + 8] = pk(final_g)
    c[:, C_GMEM:C_GMEM + 8] = pk(mem_norm_g)
    p = np.arange(128)
    r = p % 64
    invf = np.where(r < 16, 500000.0 ** (-(2.0 * (r % 8)) / 16.0), 0.0)
    hi = invf.astype(np.float32)
    c[:, C_INVF] = hi
    c[:, C_INVFLO] = (invf - hi.astype(np.float64)).astype(np.float32)
    c[:, C_IDENT:C_IDENT + 128] = np.eye(128, dtype=np.float32)
    kk, qq = np.meshgrid(p, p, indexing="ij")
    c[:, C_MASK:C_MASK + 128] = (qq >= kk).astype(np.float32)
    PT = np.zeros((128, 128), np.float32)
    for base in (0, 64):
        for i in range(8):
            PT[base + i + 8, base + i] = -1.0
            PT[base + i, base + i + 8] = 1.0
    c[:, C_PERM:C_PERM + 128] = PT
    return c


_PROG = {}


def kernel(x, mem, positions, norm_mix_g, w_in, lam_q1, lam_k1, lam_q2, lam_k2, subln_g, conv_w, w_mix_out,
           norm_x_g, mem_norm_g, w_xq, w_xkv, w_xo, norm_ffn_g, w_ff1, w_ff2, final_g):
    x = np.asarray(x, np.float32)
    mem = np.asarray(mem, np.float32)
    positions = np.asarray(positions, np.int32)
    f = lambda a: np.asarray(a, np.float32)
    wstream = prep_weights(f(w_in), f(w_mix_out), f(w_xq), f(w_xkv), f(w_xo), f(w_ff1), f(w_ff2))
    consts = prep_consts(f(norm_mix_g), f(norm_x_g), f(norm_ffn_g), f(final_g), f(mem_norm_g), f(conv_w), f(subln_g))
    lamg = np.zeros((1, DEPTH * LAMG), np.float32)
    for l in range(DEPTH):
        lamg[0, l * LAMG:(l + 1) * LAMG] = np.concatenate([f(lam_q1)[l], f(lam_k1)[l], f(lam_q2)[l], f(lam_k2)[l], f(subln_g)[l]])
    if "nc" not in _PROG:
        _PROG["nc"] = build_program()[0]
    nc = _PROG["nc"]
    in_maps = []
    for c in range(8):
        xs = x[2 * c:2 * c + 2]
        xTc = np.ascontiguousarray(xs.transpose(0, 2, 1)).reshape(NSEQ, KC, 128, SEQ)
        ms = mem[2 * c:2 * c + 2]
        mTc = np.ascontiguousarray(ms.transpose(0, 2, 1).reshape(NSEQ, KC, 128, MEM).transpose(0, 2, 1, 3)).reshape(NSEQ, 128, KC * MEM)
        in_maps.append(dict(xT=xTc, memT=mTc, pos=np.ascontiguousarray(positions[2 * c:2 * c + 2]),
                            wst=wstream, consts=consts, lamg=lamg))
    res = run_bass_kernel_spmd(nc, in_maps, core_ids=list(range(8)))
    out = np.empty((16, SEQ, D), np.float32)
    for c in range(8):
        o = np.asarray(res.results[c]["outT"]).reshape(NSEQ, D, SEQ)
        out[2 * c:2 * c + 2] = o.transpose(0, 2, 1)
    return out
```

```python
import math
from contextlib import ExitStack

import numpy as np
import concourse.bass as bass
import concourse.mybir as mybir
from concourse.bass_utils import run_bass_kernel_spmd

F32 = mybir.dt.float32
BF16 = mybir.dt.bfloat16
I32 = mybir.dt.int32
AF = mybir.ActivationFunctionType
ALU = mybir.AluOpType
AX = mybir.AxisListType

COMPUTE = ("pe", "act", "dve", "pool")


class _Rec:
    def __getattr__(self, name):
        def f(*a, **k):
            self.call = (name, a, k)
            return self
        return f


class Sched:
    def __init__(self, nc, stack):
        self.nc = nc
        self.stack = stack
        self.ops = []
        self.lw = {}
        self.rd = {}
        self.eng_obj = {"pe": nc.tensor, "act": nc.scalar, "dve": nc.vector,
                        "pool": nc.gpsimd, "sp": nc.sync}
        self.stream_ops = {}

    def add(self, eng, fn, reads=(), writes=(), stream=None):
        rec = _Rec()
        fn(rec)
        name, args, kw = rec.call
        idx = len(self.ops)
        deps = {}
        reads = list(reads)
        writes = list(writes)
        for k in reads:
            w = self.lw.get(k)
            if w is not None:
                deps[w] = "raw"
        for k in writes:
            w = self.lw.get(k)
            if w is not None and w not in deps:
                deps[w] = "waw"
            for r in self.rd.get(k, {}).values():
                if r not in deps:
                    deps[r] = "war"
        wset = set(writes)
        for k in writes:
            self.lw[k] = idx
            self.rd[k] = {}
        for k in reads:
            if k in wset:
                continue
            d = self.rd.setdefault(k, {})
            if stream is not None:
                d[("dma", idx)] = idx
            else:
                d[eng] = idx
        deps.pop(idx, None)
        self.ops.append(dict(eng=eng, name=name, args=args, kw=kw, deps=deps, stream=stream))
        if stream is not None:
            self.stream_ops.setdefault(stream, []).append(idx)
        return idx

    def _skip(self, op, dop, kind):
        if dop["eng"] == op["eng"] and op["stream"] is None and dop["stream"] is None:
            if op["eng"] == "pe":
                return True
        return False

    def emit(self, final_streams=()):
        nc = self.nc
        ops = self.ops
        signal = [False] * len(ops)
        for i, op in enumerate(ops):
            for d, kind in op["deps"].items():
                dop = ops[d]
                if dop["stream"] is not None:
                    continue
                if self._skip(op, dop, kind):
                    continue
                signal[d] = True
        esem = {e: self.stack.enter_context(nc.semaphore("sem_" + e)) for e in COMPUTE}
        ssem = {}
        for s in self.stream_ops:
            ssem[s] = self.stack.enter_context(nc.semaphore("dsem_%d" % len(ssem)))
        cnt = {e: 0 for e in COMPUTE}
        val = [0] * len(ops)
        scnt = {s: 0 for s in self.stream_ops}
        waited = {e: {} for e in self.eng_obj}
        nwait = 0
        for i, op in enumerate(ops):
            eng = op["eng"]
            eo = self.eng_obj[eng]
            need = {}
            for d, kind in op["deps"].items():
                dop = ops[d]
                if dop["stream"] is not None:
                    s = dop["stream"]
                    key = ("s", s)
                    v = 16 * scnt[s]
                else:
                    if self._skip(op, dop, kind):
                        continue
                    key = ("e", dop["eng"])
                    v = val[d]
                if v > need.get(key, 0):
                    need[key] = v
            for key, v in need.items():
                if waited[eng].get(key, 0) >= v:
                    continue
                waited[eng][key] = v
                sem = ssem[key[1]] if key[0] == "s" else esem[key[1]]
                eo.wait_ge(sem, v)
                nwait += 1
            ins = getattr(eo, op["name"])(*op["args"], **op["kw"])
            if op["stream"] is not None:
                s = op["stream"]
                scnt[s] += 1
                ins.then_inc(ssem[s], 16)
            elif signal[i]:
                cnt[eng] += 1
                val[i] = cnt[eng]
                ins.then_inc(esem[eng], 1)
            else:
                val[i] = cnt[eng]
        for s in final_streams:
            self.eng_obj["sp"].wait_ge(ssem[s], 16 * scnt[s])
        self.stats = dict(n_ops=len(ops), n_wait=nwait, cnt=cnt, n_streams=len(ssem),
                          max_stream=max(scnt.values()) if scnt else 0)
        return self.stats


D = 1024
SEQ = 2048
MEM = 256
DEPTH = 2
NSEQ = 2
KC = 8
TC = 512
NTC = SEQ // TC
NCH = 128
NSLOT = 4
SCALE_A = 64 ** -0.5
SCALE_X = 256 ** -0.5
LAMBDA_INIT = [0.8 - 0.6 * math.exp(-0.3 * l) for l in range(DEPTH)]

C_GMIX, C_GX, C_GFFN, C_GFIN, C_GMEM, C_CONVW, C_INVF, C_GSUB, C_INVFLO = 0, 16, 32, 48, 56, 64, 88, 89, 91
C_IDENT, C_MASK, C_PERM, NCONST = 96, 224, 352, 480
LAMG = 4 * 64 + 128


def build_program(n_seq=NSEQ, depth=DEPTH, stop_phase=99, final_norm=True):
    nc = bass.Bass("TRN2", target_bir_lowering=False)
    xT = nc.dram_tensor("xT", [NSEQ, KC, 128, SEQ], F32, kind="ExternalInput").ap()
    memT = nc.dram_tensor("memT", [NSEQ, 128, KC * MEM], F32, kind="ExternalInput").ap()
    pos = nc.dram_tensor("pos", [NSEQ, SEQ], I32, kind="ExternalInput").ap()
    wst = nc.dram_tensor("wst", [DEPTH * NCH, 128, KC, 128], F32, kind="ExternalInput").ap()
    consts = nc.dram_tensor("consts", [128, NCONST], F32, kind="ExternalInput").ap()
    lamg = nc.dram_tensor("lamg", [1, DEPTH * LAMG], F32, kind="ExternalInput").ap()
    outT = nc.dram_tensor("outT", [NSEQ, KC, 128, SEQ], F32, kind="ExternalOutput").ap()

    with ExitStack() as st:
        S = Sched(nc, st)
        sb = lambda n, s, d: st.enter_context(nc.sbuf_tensor(n, s, d))
        hT = sb("hT", [128, KC, SEQ], F32)
        bX = sb("bX", [128, KC, SEQ], BF16)
        bA = sb("bA", [128, KC, SEQ], BF16)
        bC = sb("bC", [128, 4, SEQ], BF16)
        bV = sb("bV", [128, 16, 516], BF16)
        ropeC = sb("ropeC", [128, SEQ], BF16)
        ropeS = sb("ropeS", [128, SEQ], BF16)
        tI = sb("tI", [128, TC], I32)
        E = [sb("E%d" % i, [128, 2 * TC], BF16) for i in range(2)]
        U = sb("U", [128, SEQ + 2], F32)
        tF = [sb("tF%d" % i, [128, TC], F32) for i in range(4)]
        wb = [sb("wb%d" % i, [128, KC, 128], BF16) for i in range(NSLOT)]
        cst = sb("cst", [128, C_IDENT], F32)
        identb = sb("identb", [128, 128], BF16)
        maskneg = sb("maskneg", [128, 128], BF16)
        permb = sb("permb", [128, 128], BF16)
        onesD = sb("onesD", [128, 128], BF16)
        ones1 = sb("ones1", [128, 128], BF16)
        eps_n = sb("eps_n", [128, 1], F32)
        eps_s = sb("eps_s", [128, 1], F32)
        neglam = sb("neglam", [128, DEPTH], F32)
        gsT = sb("gsT", [128, DEPTH], F32)
        lamt = sb("lamt", [128, 64], F32)
        lams = sb("lams", [128, 4], F32)
        Ost = sb("Ost", [128, 1032], F32)
        zr = sb("zr", [128, 8], F32)
        ssb = sb("ssb", [128, 12], F32)
        zr2 = sb("zr2", [128, 8], F32)
        ssb2 = sb("ssb2", [128, 12], F32)
        pconst = sb("pconst", [128, 12], F32)
        obw = sb("obw", [128, 512], BF16)
        sqt = sb("sqt", [128, 512], F32)
        pp = [st.enter_context(nc.psum_tensor("pp%d" % i, [128, 1024], F32)) for i in range(4)]
        ps = [pp[b // 2][:, (b % 2) * 512:(b % 2) * 512 + 512] for b in range(8)]
        ps7b = pp[3].bitcast(BF16)[:, 1024:2048]

        def tk(name, kc, c0, c1):
            return [(name, kc, j) for j in range(c0 // 128, (c1 + 127) // 128)]

        def uk(c0, c1):
            return [("U", j) for j in range(c0 // 128, (c1 + 127) // 128)]

        UALL = uk(0, SEQ + 2)
        PS = lambda b: ("ps", b)
        rr = dict(tF=0, bank=0, E=0, rq=0)

        def nxt(name, n):
            v = rr[name]
            rr[name] = (v + 1) % n
            return v

        gbs = dict(n=8, i=0)

        def gbank():
            gbs["i"] = (gbs["i"] + 1) % gbs["n"]
            return gbs["i"]

        wstate = dict(issued=0, used=0)
        twopass = stop_phase >= 99

        def layer_order(l):
            o = list(range(0, 24))
            mix = list(range(24, 32))
            o += mix + (mix if twopass else [])
            o += list(range(32, 56))
            xo = list(range(56, 64))
            o += xo + (xo if twopass else [])
            for b in range(4):
                o += list(range(64 + 16 * b, 72 + 16 * b))
                f2 = list(range(72 + 16 * b, 80 + 16 * b))
                o += f2 + (f2 if (twopass and b == 3) else [])
            return [l * NCH + i for i in o]

        worder = []
        for s_ in range(n_seq):
            for l_ in range(depth):
                worder += layer_order(l_)
        total_chunks = len(worder)

        def w_issue_upto(g):
            while wstate["issued"] <= min(g, total_chunks - 1):
                gi = wstate["issued"]
                slot = gi % NSLOT
                src = wst[worder[gi]]
                S.add("pool", (lambda slot, src: lambda e: e.dma_start(out=wb[slot][:], in_=src))(slot, src),
                      writes=[("w", slot)], stream=("w", slot))
                wstate["issued"] += 1

        def w_next(prefetch=True):
            g = wstate["used"]
            wstate["used"] += 1
            w_issue_upto(g + NSLOT - 1 if prefetch else g)
            return g % NSLOT

        def w_skip(n):
            wstate["used"] += n
            wstate["issued"] = max(wstate["issued"], wstate["used"])

        S.add("sp", lambda e: e.dma_start(out=cst[:], in_=consts[:, 0:C_IDENT]), writes=["cst"], stream="cst")
        S.add("sp", lambda e: e.dma_start(out=U[:, 0:DEPTH * LAMG], in_=lamg.partition_broadcast(128)),
              writes=uk(0, DEPTH * LAMG), stream="lamg")
        MOFF = 1024 - C_IDENT
        S.add("sp", lambda e: e.dma_start(out=U[:, 1024:1024 + NCONST - C_IDENT], in_=consts[:, C_IDENT:NCONST]),
              writes=uk(1024, 1024 + NCONST - C_IDENT), stream="cmat")
        lamg_sb = U
        LK = uk(0, DEPTH * LAMG)
        CMK = uk(1024, 1024 + NCONST - C_IDENT)
        w_issue_upto(NSLOT - 2)
        S.add("dve", lambda e: e.tensor_copy(out=identb[:], in_=U[:, MOFF + C_IDENT:MOFF + C_IDENT + 128]),
              reads=CMK, writes=["identb"])
        S.add("dve", lambda e: e.tensor_scalar(out=maskneg[:], in0=U[:, MOFF + C_MASK:MOFF + C_MASK + 128], scalar1=-1.0, scalar2=30000.0,
                                               op0=ALU.add, op1=ALU.mult), reads=CMK, writes=["maskneg"])
        S.add("dve", lambda e: e.tensor_copy(out=permb[:], in_=U[:, MOFF + C_PERM:MOFF + C_PERM + 128]),
              reads=CMK, writes=["permb"])
        S.add("dve", lambda e: e.memset(onesD[:], 1.0 / D), writes=["onesD"])
        S.add("dve", lambda e: e.memset(ones1[:], 1.0), writes=["ones1"])
        S.add("dve", lambda e: e.memset(eps_n[:], 1e-6), writes=["eps_n"])
        S.add("dve", lambda e: e.memset(eps_s[:], 1e-5), writes=["eps_s"])
        S.add("dve", lambda e: e.memset(pconst[:, 0:4], 1.0 / 128), writes=[("pconst", 0)])
        S.add("dve", lambda e: e.memset(pconst[:, 4:8], 1e-5), writes=[("pconst", 1)])
        S.add("dve", lambda e: e.memset(pconst[:, 8:12], -0.5), writes=[("pconst", 2)])
        S.add("dve", lambda e: e.memset(bV[:], 1.0), writes=[("bV", t) for t in range(16)])
        for l in range(depth):
            base = l * LAMG
            for i in range(2):
                S.add("dve", (lambda base, i: lambda e: e.tensor_tensor(
                    out=lamt[:], in0=lamg_sb[:, base + 128 * i: base + 128 * i + 64],
                    in1=lamg_sb[:, base + 128 * i + 64: base + 128 * i + 128], op=ALU.mult))(base, i),
                    reads=LK, writes=["lamt"])
                S.add("dve", (lambda i: lambda e: e.tensor_reduce(out=lams[:, i:i + 1], in_=lamt[:], axis=AX.X, op=ALU.add))(i),
                      reads=["lamt"], writes=[("lams", i)])
                S.add("act", (lambda i: lambda e: e.activation(out=lams[:, 2 + i:3 + i], in_=lams[:, i:i + 1], func=AF.Exp))(i),
                      reads=[("lams", i)], writes=[("lams", 2 + i)])
            S.add("dve", (lambda l: lambda e: e.tensor_tensor(out=neglam[:, l:l + 1], in0=lams[:, 3:4], in1=lams[:, 2:3],
                                                            op=ALU.subtract))(l),
                  reads=[("lams", 2), ("lams", 3)], writes=[("neglam", l)])
            S.add("dve", (lambda l: lambda e: e.tensor_scalar_add(out=neglam[:, l:l + 1], in0=neglam[:, l:l + 1],
                                                                scalar1=-LAMBDA_INIT[l]))(l),
                  reads=[("neglam", l)], writes=[("neglam", l)])
            S.add("dve", lambda e: e.tensor_scalar_mul(out=gsT[:, l:l + 1], in0=cst[:, C_GSUB + l:C_GSUB + l + 1], scalar1=1.0 - LAMBDA_INIT[l]),
                  reads=["cst"], writes=[("gsT", l)])

        def act_ap(buf, k, c0, cn):
            return buf[:, k, c0:c0 + cn]

        def gemm(nchunks, act_of_k, act_keys_of_k, tcols, evac, tcn=TC, t_lo=0, ci0=0):
            for ci in range(ci0, ci0 + nchunks):
                slot = w_next()
                for t0 in range(t_lo, tcols, tcn):
                    bank = gbank()
                    for k in range(KC):
                        S.add("pe", (lambda bank, slot, k, t0: lambda e: e.matmul(
                            ps[bank][:, 0:tcn], lhsT=wb[slot][:, k, :], rhs=act_of_k(k, t0, tcn),
                            start=(k == 0), stop=(k == KC - 1)))(bank, slot, k, t0),
                            reads=[("w", slot)] + act_keys_of_k(k, t0, tcn), writes=[PS(bank)])
                    evac(ci, t0, bank)

        late_norm = []

        def flush_late():
            while late_norm:
                late_norm.pop(0)()

        def gemm_multi(evacs, act_of_k, act_keys_of_k, halves, ci0=0):
            slots = [w_next(prefetch=(i == 0)) for i in range(len(evacs))]
            for hi_, (lo, hi) in enumerate(halves):
                if hi_ == 1:
                    flush_late()
                for t0 in range(lo, hi, TC):
                    for i, ev in enumerate(evacs):
                        bank = gbank()
                        slot = slots[i]
                        for k in range(KC):
                            S.add("pe", lambda e: e.matmul(ps[bank][:, 0:TC], lhsT=wb[slot][:, k, :], rhs=act_of_k(k, t0, TC),
                                                           start=(k == 0), stop=(k == KC - 1)),
                                  reads=[("w", slot)] + act_keys_of_k(k, t0, TC), writes=[PS(bank)])
                        ev(ci0 + i, t0, bank)

        def rms_norm(src_of_k, src_keys_of_k, gcol, dst_of_k, dst_keys_of_k, ncols, cn, src_wide=None, t_lo=0, part="all"):
            if part == "sq":
                for t0 in range(t_lo, ncols, cn):
                    for k in range(KC):
                        S.add("act", lambda e: e.activation(out=bA[:, k, t0:t0 + cn], in_=src_of_k(k, t0, cn), func=AF.Square),
                              reads=src_keys_of_k(k, t0, cn), writes=tk("bA", k, t0, t0 + cn))
                return
            for t0 in range(t_lo, ncols, cn):
                bank = gbank()
                if src_wide is not None:
                    S.add("act", lambda e: e.activation(out=bA[:, :, t0:t0 + cn], in_=src_wide(t0, cn), func=AF.Square),
                          reads=[k_ for k in range(KC) for k_ in src_keys_of_k(k, t0, cn)],
                          writes=[k_ for k in range(KC) for k_ in tk("bA", k, t0, t0 + cn)])
                for k in range(KC):
                    if src_wide is None and part == "all":
                        S.add("act", lambda e: e.activation(out=bA[:, k, t0:t0 + cn], in_=src_of_k(k, t0, cn), func=AF.Square),
                              reads=src_keys_of_k(k, t0, cn), writes=tk("bA", k, t0, t0 + cn))
                    S.add("pe", lambda e: e.matmul(ps[bank][:, 0:cn], lhsT=onesD[:], rhs=bA[:, k, t0:t0 + cn],
                                                   start=(k == 0), stop=(k == KC - 1)),
                          reads=tk("bA", k, t0, t0 + cn) + ["onesD"], writes=[PS(bank)])
                f = nxt("tF", 4)
                S.add("act", lambda e: e.activation(out=tF[f][:, 0:cn], in_=ps[bank][:, 0:cn], func=AF.Ln, bias=eps_n[:, 0:1], scale=1.0),
                      reads=["eps_n"], writes=[PS(bank), ("tF", f)])
                S.add("act", lambda e: e.activation(out=tF[f][:, 0:cn], in_=tF[f][:, 0:cn], func=AF.Exp, scale=-0.5),
                      reads=[("tF", f)], writes=[("tF", f)])
                for k in range(KC):
                    S.add("dve", (lambda f, k, t0: lambda e: e.scalar_tensor_tensor(
                        out=dst_of_k(k, t0, cn), in0=src_of_k(k, t0, cn), scalar=cst[:, gcol + k:gcol + k + 1],
                        in1=tF[f][:, 0:cn], op0=ALU.mult, op1=ALU.mult))(f, k, t0),
                        reads=src_keys_of_k(k, t0, cn) + [("tF", f), "cst"], writes=dst_keys_of_k(k, t0, cn))

        h_wide = lambda t0, cn: hT[:, :, t0:t0 + cn]
        h_of = lambda k, t0, cn: hT[:, k, t0:t0 + cn]
        h_keys = lambda k, t0, cn: tk("h", k, t0, t0 + cn)
        bX_of = lambda k, t0, cn: bX[:, k, t0:t0 + cn]
        bX_keys = lambda k, t0, cn: tk("bX", k, t0, t0 + cn)
        bA_of = lambda k, t0, cn: bA[:, k, t0:t0 + cn]
        bA_keys = lambda k, t0, cn: tk("bA", k, t0, t0 + cn)

        def resid_evac(ci, t0, bank):
            S.add("dve", lambda e: e.tensor_tensor(out=hT[:, ci, t0:t0 + TC], in0=ps[bank][:, 0:TC], in1=hT[:, ci, t0:t0 + TC], op=ALU.add),
                  reads=tk("h", ci, t0, t0 + TC), writes=[PS(bank)] + tk("h", ci, t0, t0 + TC))

        sin_pending = []

        def flush_sin():
            while sin_pending:
                sin_pending.pop(0)()

        def rope_setup(s, t0, use_tF=False):
            if use_tF:
                T = [Ost[:, 0:TC], Ost[:, 516:516 + TC], sqt[:, :], tF[3][:]]
                TK = [[("Ost", n_) for n_ in range(4)], [("Ost", n_) for n_ in range(4, 8)], ["sqt"], [("tF", 3)]]
            else:
                T = [U[:, i * TC:(i + 1) * TC] for i in range(4)]
                TK = [uk(i * TC, (i + 1) * TC) for i in range(4)]
            IK = ["tI"]
            S.add("sp", lambda e: e.dma_start(out=tI[:], in_=pos[s:s + 1, t0:t0 + TC].partition_broadcast(128)), writes=IK, stream="pos")
            S.add("dve", lambda e: e.tensor_copy(out=T[0], in_=tI[:]), reads=IK, writes=TK[0])
            S.add("dve", lambda e: e.tensor_scalar_mul(out=T[1], in0=T[0], scalar1=cst[:, C_INVF:C_INVF + 1]), reads=TK[0] + ["cst"], writes=TK[1])
            S.add("dve", lambda e: e.scalar_tensor_tensor(out=T[1], in0=T[0], scalar=cst[:, C_INVFLO:C_INVFLO + 1], in1=T[1],
                                                          op0=ALU.mult, op1=ALU.add), reads=TK[0] + TK[1] + ["cst"], writes=TK[1])
            S.add("dve", lambda e: e.tensor_scalar_mul(out=T[1], in0=T[1], scalar1=float(1.0 / (2 * np.pi))), reads=TK[1], writes=TK[1])
            for (phase, tab, tname, fu, fk) in ((0.0, ropeS, "ropeS", 2, 0), (0.25, ropeC, "ropeC", 3, 0)):
                S.add("dve", lambda e: e.tensor_scalar_add(out=T[fu], in0=T[1], scalar1=phase), reads=TK[1], writes=TK[fu])
                S.add("dve", lambda e: e.tensor_copy(out=tI[:], in_=T[fu]), reads=TK[fu], writes=IK)
                S.add("dve", lambda e: e.tensor_copy(out=T[fk], in_=tI[:]), reads=IK, writes=TK[fk])
                S.add("dve", lambda e: e.tensor_tensor(out=T[fu], in0=T[fu], in1=T[fk], op=ALU.subtract), reads=TK[fu] + TK[fk], writes=TK[fu])
                S.add("dve", lambda e: e.tensor_single_scalar(out=T[fk], in_=T[fu], scalar=0.5, op=ALU.is_gt), reads=TK[fu], writes=TK[fk])
                S.add("dve", lambda e: e.tensor_tensor(out=T[fu], in0=T[fu], in1=T[fk], op=ALU.subtract), reads=TK[fu] + TK[fk], writes=TK[fu])
                sin_pending.append(lambda tab=tab, tname=tname, fu=fu: S.add(
                    "act", lambda e: e.activation(out=tab[:, t0:t0 + TC], in_=T[fu], func=AF.Sin, scale=float(2 * np.pi)),
                    reads=TK[fu], writes=[(tname, t0 // TC)]))

        def resid_gemm(act_of, act_keys, norm_fn, inject=None):
            if not twopass or norm_fn is None:
                gemm(8, act_of, act_keys, SEQ, resid_evac)
                if twopass:
                    raise RuntimeError("weight order expects two passes")
                return False
            H = SEQ // 2
            gemm(2, act_of, act_keys, H, resid_evac)
            if inject is not None:
                inject()
            gemm(6, act_of, act_keys, H, resid_evac, ci0=2)
            norm_fn(0, H, "sq")
            gemm(2, act_of, act_keys, SEQ, resid_evac, t_lo=H)
            norm_fn(0, H, "rest")
            gemm(6, act_of, act_keys, SEQ, resid_evac, t_lo=H, ci0=2)
            norm_fn(H, SEQ, "sq")
            late_norm.append(lambda: norm_fn(H, SEQ, "rest"))
            return True

        def final_norm_range(s, t_lo, t_hi, part="all"):
            if part == "sq":
                for t0 in range(t_lo, t_hi, TC):
                    for k in range(KC):
                        S.add("act", lambda e: e.activation(out=bA[:, k, t0:t0 + TC], in_=hT[:, k, t0:t0 + TC], func=AF.Square),
                              reads=tk("h", k, t0, t0 + TC), writes=tk("bA", k, t0, t0 + TC))
                return
            for t0 in range(t_lo, t_hi, TC):
                bank = gbank()
                for k in range(KC):
                    if part == "all":
                        S.add("act", lambda e: e.activation(out=bA[:, k, t0:t0 + TC], in_=hT[:, k, t0:t0 + TC], func=AF.Square),
                              reads=tk("h", k, t0, t0 + TC), writes=tk("bA", k, t0, t0 + TC))
                    S.add("pe", lambda e: e.matmul(ps[bank][:, 0:TC], lhsT=onesD[:], rhs=bA[:, k, t0:t0 + TC],
                                                   start=(k == 0), stop=(k == KC - 1)),
                          reads=tk("bA", k, t0, t0 + TC) + ["onesD"], writes=[PS(bank)])
                f = nxt("tF", 4)
                S.add("act", lambda e: e.activation(out=tF[f][:], in_=ps[bank][:, 0:TC], func=AF.Ln, bias=eps_n[:, 0:1], scale=1.0),
                      reads=["eps_n"], writes=[PS(bank), ("tF", f)])
                S.add("act", lambda e: e.activation(out=tF[f][:], in_=tF[f][:], func=AF.Exp, scale=-0.5),
                      reads=[("tF", f)], writes=[("tF", f)])
                for k in range(KC):
                    S.add("dve", lambda e: e.scalar_tensor_tensor(
                        out=hT[:, k, t0:t0 + TC], in0=hT[:, k, t0:t0 + TC], scalar=cst[:, C_GFIN + k:C_GFIN + k + 1],
                        in1=tF[f][:], op0=ALU.mult, op1=ALU.mult),
                        reads=tk("h", k, t0, t0 + TC) + [("tF", f), "cst"], writes=tk("h", k, t0, t0 + TC))
                    S.add("sp", lambda e: e.dma_start(out=outT[s, k, :, t0:t0 + TC], in_=hT[:, k, t0:t0 + TC]),
                          reads=tk("h", k, t0, t0 + TC), stream=("hout", k, t0))

        out_streams = []
        for s in range(n_seq):
            for t0 in range(0, SEQ, TC):
                for k in range(KC):
                    S.add("sp", lambda e: e.dma_start(out=hT[:, k, t0:t0 + TC], in_=xT[s, k, :, t0:t0 + TC]),
                          writes=tk("h", k, t0, t0 + TC), stream=("hin", k, t0))

            p1_done = False
            fin_done = False
            for l in range(depth):
                ph = 10 * l
                if not p1_done:
                    rms_norm(h_of, h_keys, C_GMIX + 8 * l, bX_of, bX_keys, SEQ, TC)
                p1_done = False
                if stop_phase <= ph + 1:
                    w_skip(NCH)
                    continue

                if not twopass:
                    flush_late()
                S.add("dve", lambda e: e.memset(U[:, 0:2], 0.0), writes=uk(0, 2))
                for cc in range(4):
                    def ev_h(ci, t0, bank):
                        S.add("act", lambda e: e.activation(out=U[:, 2 + t0:2 + t0 + TC], in_=ps[bank][:, 0:TC], func=AF.Copy),
                              writes=[PS(bank)] + uk(2 + t0, 2 + t0 + TC))

                    def ev_c(ci, t0, bank):
                        S.add("dve", lambda e: e.tensor_tensor(out=U[:, 2 + t0:2 + t0 + TC], in0=ps[bank][:, 0:TC],
                                                               in1=U[:, 2 + t0:2 + t0 + TC], op=ALU.mult),
                              reads=uk(2 + t0, 2 + t0 + TC), writes=[PS(bank)] + uk(2 + t0, 2 + t0 + TC))

                    def ev_b(ci, t0, bank, cc=cc):
                        wc = C_CONVW + (l * 4 + cc) * 3
                        f = nxt("tF", 4)
                        S.add("act", lambda e: e.activation(out=tF[f][:], in_=U[:, 2 + t0:2 + t0 + TC], func=AF.Copy,
                                                            scale=cst[:, wc + 2:wc + 3]),
                              reads=uk(2 + t0, 2 + t0 + TC) + ["cst"], writes=[("tF", f)])
                        S.add("dve", lambda e: e.scalar_tensor_tensor(out=tF[f][:], in0=U[:, 1 + t0:1 + t0 + TC], scalar=cst[:, wc + 1:wc + 2],
                                                                      in1=tF[f][:], op0=ALU.mult, op1=ALU.add),
                              reads=uk(1 + t0, 1 + t0 + TC) + ["cst", ("tF", f)], writes=[("tF", f)])
                        S.add("dve", lambda e: e.scalar_tensor_tensor(out=tF[f][:], in0=U[:, t0:t0 + TC], scalar=cst[:, wc:wc + 1],
                                                                      in1=tF[f][:], op0=ALU.mult, op1=ALU.add),
                              reads=uk(t0, t0 + TC) + ["cst", ("tF", f)], writes=[("tF", f)])
                        S.add("dve", lambda e: e.tensor_tensor(out=bC[:, cc, t0:t0 + TC], in0=ps[bank][:, 0:TC], in1=tF[f][:], op=ALU.mult),
                              reads=[("tF", f)], writes=[PS(bank)] + tk("bC", cc, t0, t0 + TC))

                    if twopass and cc == 0:
                        gemm_multi([ev_h, ev_c, ev_b], bX_of, bX_keys, [(0, SEQ // 2), (SEQ // 2, SEQ)])
                    else:
                        gemm(1, bX_of, bX_keys, SEQ, ev_h)
                        gemm(1, bX_of, bX_keys, SEQ, ev_c)
                        flush_sin()
                        gemm(1, bX_of, bX_keys, SEQ, ev_b)
                    if s == 0 and l == 0:
                        rope_setup(0, cc * TC, use_tF=True)

                for hh in range(4):
                    slot = w_next()
                    for g in range(4):
                        bank = gbank()
                        for j in range(4):
                            tt = 4 * g + j
                            for k in range(KC):
                                S.add("pe", (lambda bank, slot, k, j, tt: lambda e: e.matmul(
                                    ps[bank][:, j * 128:(j + 1) * 128], lhsT=bX[:, k, tt * 128:(tt + 1) * 128], rhs=wb[slot][:, k, :],
                                    start=(k == 0), stop=(k == KC - 1)))(bank, slot, k, j, tt),
                                    reads=[("w", slot), ("bX", k, tt)], writes=[PS(bank)])
                        S.add("act", (lambda bank, g, hh: lambda e: e.activation(
                            out=bV[:, 4 * g:4 * g + 4, hh * 129:hh * 129 + 128],
                            in_=ps[bank][:, :].rearrange("p (j c) -> p j c", j=4), func=AF.Copy))(bank, g, hh),
                            writes=[PS(bank)] + [("bV", 4 * g + j) for j in range(4)])

                flush_sin()
                qk_pending = []
                gbs["n"] = 6
                for hh in range(4):
                    for which in range(2):
                        dstc = hh + 4 * which

                        def ev_qk(ci, t0, bank, dstc=dstc):
                            b = nxt("E", 2)
                            rb = 6 + nxt("rq", 2)
                            while qk_pending:
                                qk_pending.pop(0)()
                            S.add("act", lambda e: e.activation(out=E[b][:, 0:TC], in_=ps[bank][:, 0:TC], func=AF.Copy),
                                  writes=[PS(bank), ("E", b, 0)])

                            def stage_b():
                                S.add("pe", lambda e: e.matmul(ps[rb][:, 0:TC], lhsT=permb[:], rhs=E[b][:, 0:TC], start=True, stop=True),
                                      reads=[("E", b, 0), "permb"], writes=[PS(rb)])
                                S.add("dve", lambda e: e.tensor_tensor(out=bA[:, dstc, t0:t0 + TC], in0=ps[bank][:, 0:TC], in1=ropeC[:, t0:t0 + TC], op=ALU.mult),
                                      reads=[("ropeC", t0 // TC)], writes=[PS(bank)] + tk("bA", dstc, t0, t0 + TC))
                                S.add("dve", lambda e: e.tensor_tensor(out=E[b][:, TC:2 * TC], in0=ps[rb][:, 0:TC], in1=ropeS[:, t0:t0 + TC], op=ALU.mult),
                                      reads=[("ropeS", t0 // TC)], writes=[PS(rb), ("E", b, 1)])
                                S.add("dve", lambda e: e.tensor_tensor(out=bA[:, dstc, t0:t0 + TC], in0=bA[:, dstc, t0:t0 + TC], in1=E[b][:, TC:2 * TC], op=ALU.add),
                                      reads=[("E", b, 1)], writes=tk("bA", dstc, t0, t0 + TC))
                            qk_pending.append(stage_b)

                        gemm(1, bX_of, bX_keys, SEQ, ev_qk)
                while qk_pending:
                    qk_pending.pop(0)()
                gbs["n"] = 8
                if stop_phase <= ph + 2:
                    w_skip(NCH - 24)
                    continue

                steps = []
                for hh in range(4):
                    for qc in range(4):
                        J = 4 * qc + 4
                        for j in range(J):
                            steps.append((hh, qc, j, j == J - 1))
                pending = []

                def emit_qk(n):
                    hh, qc, j, last = steps[n]
                    q0 = max(TC * qc, 128 * j)
                    qn = TC * (qc + 1) - q0
                    sb2 = 2 * (n % 2)
                    diag = 128 * j >= TC * qc
                    for c in range(2):
                        S.add("pe", lambda e: e.matmul(
                            ps[sb2 + c][:, 0:qn], lhsT=bA[c * 64:(c + 1) * 64, 4 + hh, 128 * j:128 * j + 128],
                            rhs=bA[c * 64:(c + 1) * 64, hh, q0:q0 + qn], start=True, stop=not diag),
                            reads=[("bA", 4 + hh, j)] + tk("bA", hh, q0, q0 + qn), writes=[PS(sb2 + c)])
                        if diag:
                            S.add("pe", lambda e: e.matmul(ps[sb2 + c][:, 0:128], lhsT=identb[:], rhs=maskneg[:], start=False, stop=True),
                                  reads=["identb", "maskneg"], writes=[PS(sb2 + c)])

                def bc_ap(t, col0, n_mid, n_in, mid_stride, in_stride):
                    return bass.AP(t, col0, [[t.shape[1], 128], [mid_stride, n_mid], [in_stride, n_in]])

                STG = [Ost, U]
                O0w = [t[:, 0:516].rearrange("p (n c) -> p n c", c=129)[:, :, 0:128] for t in STG]
                O1w = [t[:, 516:1032].rearrange("p (n c) -> p n c", c=129)[:, :, 0:128] for t in STG]
                O3 = [t[:, 0:1032].rearrange("p (n c) -> p n c", c=129) for t in STG]
                OK0 = [[("Ost", n_) for n_ in range(4)], uk(0, 1032)]
                OK1 = [[("Ost", n_) for n_ in range(4, 8)], uk(0, 1032)]
                OKB = [[[("Ost", n_) for n_ in (0, 1, 2)], [("Ost", n_) for n_ in (3, 4, 5)], [("Ost", n_) for n_ in (6, 7)]],
                       [uk(0, 1032)] * 3]
                zrs = [zr, zr2]
                ssbs = [ssb, ssb2]

                def part1(si):
                    T = STG[si]
                    S.add("dve", lambda e: e.tensor_copy(out=T[:, 0:387], in_=ps[4][:, 0:387]), writes=[PS(4)] + OKB[si][0])
                    S.add("dve", lambda e: e.tensor_copy(out=T[:, 387:774], in_=ps[5][:, 0:387]), writes=[PS(5)] + OKB[si][1])
                    S.add("dve", lambda e: e.tensor_copy(out=T[:, 774:1032], in_=ps[6][:, 0:258]), writes=[PS(6)] + OKB[si][2])

                def part2a(hh, qc, si):
                    z_, sb_ = zrs[si], ssbs[si]
                    S.add("dve", lambda e: e.reciprocal(out=z_[:, 0:8], in_=O3[si][:, :, 128]), reads=OK0[si] + OK1[si], writes=[("zr", si)])
                    S.add("dve", lambda e: e.tensor_scalar_mul(out=z_[:, 4:8], in0=z_[:, 4:8], scalar1=neglam[:, l:l + 1]),
                          reads=[("zr", si), ("neglam", l)], writes=[("zr", si)])
                    S.add("dve", lambda e: e.tensor_tensor(out=O0w[si], in0=O0w[si], in1=bc_ap(z_, 0, 4, 128, 1, 0), op=ALU.mult),
                          reads=[("zr", si)], writes=OK0[si])
                    S.add("dve", lambda e: e.tensor_tensor(out=O1w[si], in0=O1w[si], in1=bc_ap(z_, 4, 4, 128, 1, 0), op=ALU.mult),
                          reads=[("zr", si)], writes=OK1[si])
                    S.add("dve", lambda e: e.tensor_tensor(out=O0w[si], in0=O0w[si], in1=O1w[si], op=ALU.add), reads=OK1[si], writes=OK0[si])
                    S.add("dve", lambda e: e.tensor_tensor(out=sqt[:, :].rearrange("p (n c) -> p n c", c=128), in0=O0w[si], in1=O0w[si], op=ALU.mult),
                          reads=OK0[si], writes=["sqt"])
                    S.add("dve", lambda e: e.tensor_reduce(out=sb_[:, 0:4], in_=sqt[:, :].rearrange("p (n c) -> p n c", c=128), axis=AX.X, op=ALU.add),
                          reads=["sqt"], writes=[("ssb", si)])

                def part2b(hh, qc, si):
                    sb_ = ssbs[si]
                    S.add("pool", lambda e: e.tensor_tensor(out=sb_[:, 4:8], in0=sb_[:, 0:4], in1=pconst[:, 0:4], op=ALU.mult),
                          reads=[("ssb", si), ("pconst", 0)], writes=[("ssb_ln", si)])
                    S.add("pool", lambda e: e.tensor_tensor(out=sb_[:, 4:8], in0=sb_[:, 4:8], in1=pconst[:, 4:8], op=ALU.add),
                          reads=[("ssb_ln", si), ("pconst", 1)], writes=[("ssb_ln", si)])
                    S.add("pool", lambda e: e.tensor_tensor(out=sb_[:, 8:12], in0=sb_[:, 4:8], in1=pconst[:, 8:12], op=ALU.pow),
                          reads=[("ssb_ln", si), ("pconst", 2)], writes=[("ssb_r", si)])
                    S.add("dve", lambda e: e.tensor_tensor(out=obw[:, :].rearrange("p (n c) -> p n c", c=128), in0=O0w[si],
                                                           in1=bc_ap(sb_, 8, 4, 128, 1, 0), op=ALU.mult),
                          reads=OK0[si] + [("ssb_r", si)], writes=["obw"])

                def part2c(hh, qc):
                    for a in range(4):
                        S.add("pe", lambda e: e.transpose(ps7b[:, a * 128:(a + 1) * 128], obw[:, a * 128:(a + 1) * 128], identb[:]),
                              reads=["obw", "identb"], writes=[PS(7)])
                    S.add("dve", lambda e: e.tensor_scalar_mul(out=bX[:, hh, TC * qc:TC * (qc + 1)], in0=ps7b[:, 0:TC], scalar1=gsT[:, l:l + 1]),
                          reads=[("gsT", l)], writes=[PS(7)] + tk("bX", hh, TC * qc, TC * (qc + 1)))

                def flush(upto):
                    keep = []
                    for due, fn, a_ in pending:
                        if due <= upto:
                            fn(*a_)
                        else:
                            keep.append((due, fn, a_))
                    pending[:] = keep

                emit_qk(0)
                emit_qk(1)
                first_touch = set()
                chunk_no = [0]
                for n in range(len(steps)):
                    hh, qc, j, last = steps[n]
                    if j == 0:
                        first_touch = set()
                    q0 = max(TC * qc, 128 * j)
                    qn = TC * (qc + 1) - q0
                    eb = n % 2
                    sb2 = 2 * eb
                    S.add("act", lambda e: e.activation(out=E[eb][:, :].rearrange("p (c q) -> p c q", c=2)[:, :, 0:qn],
                                                        in_=pp[eb][:, :].rearrange("p (c q) -> p c q", c=2)[:, :, 0:qn], func=AF.Exp, scale=SCALE_A),
                          writes=[PS(sb2), PS(sb2 + 1), ("E", eb, 0), ("E", eb, 1)])
                    if n + 2 < len(steps):
                        emit_qk(n + 2)
                    for i in range(max(4 * qc, j), 4 * qc + 4):
                        qi0 = 128 * i - q0
                        a = i - 4 * qc
                        for c in range(2):
                            nacc = c * 4 + a
                            ob_ = 4 + nacc // 3
                            col = (nacc % 3) * 129
                            st_ = ob_ not in first_touch
                            first_touch.add(ob_)
                            S.add("pe", lambda e: e.matmul(
                                ps[ob_][:, col:col + 129], lhsT=E[eb][:, c * TC + qi0:c * TC + qi0 + 128],
                                rhs=bV[:, j, hh * 129:(hh + 1) * 129], start=st_, stop=(j == i), skip_group_check=True),
                                reads=[("E", eb, c), ("bV", j)], writes=[PS(ob_)])
                    flush(n)
                    if last:
                        si = chunk_no[0] % 2
                        chunk_no[0] += 1
                        part1(si)
                        part2a(hh, qc, si)
                        pending.append((n + 6, part2b, (hh, qc, si)))
                        pending.append((n + 9, part2c, (hh, qc)))
                late_pe = []
                for due, fn, a_ in pending:
                    if fn is part2c and twopass:
                        late_pe.append((fn, a_))
                    else:
                        fn(*a_)
                pending[:] = []
                if stop_phase <= ph + 3:
                    w_skip(NCH - 24)
                    continue

                mix_of = lambda k, t0, cn: (bX[:, k, t0:t0 + cn] if k < 4 else bC[:, k - 4, t0:t0 + cn])
                mix_keys = lambda k, t0, cn: (tk("bX", k, t0, t0 + cn) if k < 4 else tk("bC", k - 4, t0, t0 + cn))
                n5 = resid_gemm(mix_of, mix_keys, (lambda lo, hi, part, l=l: rms_norm(h_of, h_keys, C_GX + 8 * l, bX_of, bX_keys, hi, TC, t_lo=lo, part=part)) if twopass else None,
                                inject=lambda: [fn(*a_) for fn, a_ in late_pe]) \
                    if twopass else (gemm(8, mix_of, mix_keys, SEQ, resid_evac) or False)
                if stop_phase <= ph + 4:
                    w_skip(NCH - 32)
                    continue

                flush_late()
                if not n5:
                    rms_norm(h_of, h_keys, C_GX + 8 * l, bX_of, bX_keys, SEQ, TC)
                S.add("sp", (lambda s: lambda e: e.dma_start(out=U[:, 0:KC * MEM], in_=memT[s]))(s),
                      writes=uk(0, KC * MEM), stream="mem")
                m_of = lambda k, t0, cn: U[:, k * MEM + t0:k * MEM + t0 + cn]
                m_keys = lambda k, t0, cn: uk(k * MEM + t0, k * MEM + t0 + cn)
                mn_of = lambda k, t0, cn: bC[:, 2, k * MEM + t0:k * MEM + t0 + cn]
                mn_keys = lambda k, t0, cn: tk("bC", 2, k * MEM + t0, k * MEM + t0 + cn)
                rms_norm(m_of, m_keys, C_GMEM, mn_of, mn_keys, MEM, MEM)

                for c8 in range(8):
                    def ev_xq(ci, t0, bank, c8=c8):
                        S.add("act", lambda e: e.activation(out=bA[:, c8, t0:t0 + TC], in_=ps[bank][:, 0:TC], func=AF.Copy),
                              writes=[PS(bank)] + tk("bA", c8, t0, t0 + TC))
                    gemm(1, bX_of, bX_keys, SEQ, ev_xq)

                    def ev_xk(ci, t0, bank, c8=c8):
                        S.add("act", lambda e: e.activation(out=bC[:, 0, c8 * MEM:(c8 + 1) * MEM], in_=ps[bank][:, 0:MEM], func=AF.Copy),
                              writes=[PS(bank)] + tk("bC", 0, c8 * MEM, (c8 + 1) * MEM))
                    gemm(1, mn_of, mn_keys, MEM, ev_xk, tcn=MEM)
                    slot = w_next()
                    bank = gbank()
                    for mt in range(2):
                        for k in range(KC):
                            S.add("pe", lambda e: e.matmul(
                                ps[bank][:, mt * 128:(mt + 1) * 128], lhsT=bC[:, 2, k * MEM + mt * 128:k * MEM + (mt + 1) * 128],
                                rhs=wb[slot][:, k, :], start=(k == 0), stop=(k == KC - 1)),
                                reads=[("w", slot), ("bC", 2, 2 * k + mt)], writes=[PS(bank)])
                    S.add("act", lambda e: e.activation(
                        out=bC[:, 1, :].rearrange("p (m c) -> p m c", m=2)[:, :, c8 * 128:(c8 + 1) * 128],
                        in_=ps[bank][:, 0:256].rearrange("p (m c) -> p m c", m=2), func=AF.Copy),
                        writes=[PS(bank), ("bC", 1, c8), ("bC", 1, 8 + c8)])
                xsteps = [(hh, t0) for hh in range(4) for t0 in range(0, SEQ, TC)]

                def x_qk(n):
                    hh, t0 = xsteps[n]
                    eb = n % 2
                    for mt in range(2):
                        bank = 2 * eb + mt
                        for dc in range(2):
                            S.add("pe", lambda e: e.matmul(
                                ps[bank][:, 0:TC], lhsT=bC[:, 0, (2 * hh + dc) * MEM + mt * 128:(2 * hh + dc) * MEM + (mt + 1) * 128],
                                rhs=bA[:, 2 * hh + dc, t0:t0 + TC], start=(dc == 0), stop=(dc == 1)),
                                reads=[("bC", 0, 2 * (2 * hh + dc) + mt)] + tk("bA", 2 * hh + dc, t0, t0 + TC), writes=[PS(bank)])

                x_qk(0)
                x_qk(1)
                for n in range(len(xsteps)):
                    hh, t0 = xsteps[n]
                    eb = n % 2
                    for mt in range(2):
                        bank = 2 * eb + mt
                        S.add("act", lambda e: e.activation(out=E[eb][:, mt * TC:(mt + 1) * TC], in_=ps[bank][:, 0:TC], func=AF.Exp, scale=SCALE_X),
                              writes=[PS(bank), ("E", eb, mt)])
                    for mt in range(2):
                        S.add("pe", lambda e: e.matmul(ps[6][:, 0:TC], lhsT=ones1[:], rhs=E[eb][:, mt * TC:(mt + 1) * TC],
                                                       start=(mt == 0), stop=(mt == 1)),
                              reads=["ones1", ("E", eb, mt)], writes=[PS(6)])
                    if n + 2 < len(xsteps):
                        x_qk(n + 2)
                    for dvc in range(2):
                        for mt in range(2):
                            S.add("pe", lambda e: e.matmul(
                                ps[4 + dvc][:, 0:TC], lhsT=bC[:, 1, mt * 1024 + hh * 256 + dvc * 128:mt * 1024 + hh * 256 + (dvc + 1) * 128],
                                rhs=E[eb][:, mt * TC:(mt + 1) * TC], start=(mt == 0), stop=(mt == 1)),
                                reads=[("bC", 1, mt * 8 + 2 * hh + dvc), ("E", eb, mt)], writes=[PS(4 + dvc)])
                    f = nxt("tF", 4)
                    S.add("act", lambda e: e.activation(out=tF[f][:], in_=ps[6][:, 0:TC], func=AF.Ln), writes=[PS(6), ("tF", f)])
                    S.add("act", lambda e: e.activation(out=tF[f][:], in_=tF[f][:], func=AF.Exp, scale=-1.0), reads=[("tF", f)], writes=[("tF", f)])
                    for dvc in range(2):
                        S.add("dve", lambda e: e.tensor_tensor(out=bX[:, 2 * hh + dvc, t0:t0 + TC], in0=ps[4 + dvc][:, 0:TC],
                                                               in1=tF[f][:], op=ALU.mult),
                              reads=[("tF", f)], writes=[PS(4 + dvc)] + tk("bX", 2 * hh + dvc, t0, t0 + TC))
                n7 = resid_gemm(bX_of, bX_keys, (lambda lo, hi, part, l=l: rms_norm(h_of, h_keys, C_GFFN + 8 * l, bX_of, bX_keys, hi, TC, t_lo=lo, part=part))) \
                    if twopass else (gemm(8, bX_of, bX_keys, SEQ, resid_evac) or False)
                if stop_phase <= ph + 6:
                    w_skip(NCH - 72)
                    continue

                if not n7:
                    rms_norm(h_of, h_keys, C_GFFN + 8 * l, bX_of, bX_keys, SEQ, TC)
                for blk in range(4):
                    def ev_f1(ci, t0, bank):
                        f = nxt("tF", 4)
                        S.add("act", lambda e: e.activation(out=tF[f][:], in_=ps[bank][:, 0:TC], func=AF.Relu),
                              writes=[PS(bank), ("tF", f)])
                        S.add("dve", lambda e: e.tensor_tensor(out=bA[:, ci, t0:t0 + TC], in0=tF[f][:], in1=tF[f][:], op=ALU.mult),
                              reads=[("tF", f)], writes=tk("bA", ci, t0, t0 + TC))
                    if twopass and blk == 0:
                        gemm_multi([ev_f1] * 3, bX_of, bX_keys, [(0, SEQ // 2), (SEQ // 2, SEQ)])
                        gemm(5, bX_of, bX_keys, SEQ, ev_f1, ci0=3)
                    else:
                        gemm(8, bX_of, bX_keys, SEQ, ev_f1)
                    if l == depth - 1 and s + 1 < n_seq:
                        rope_setup(s + 1, blk * TC)
                    if twopass and blk == 3:
                        if l + 1 < depth:
                            nfn = lambda lo, hi, part, l=l: rms_norm(h_of, h_keys, C_GMIX + 8 * (l + 1), bX_of, bX_keys, hi, TC, t_lo=lo, part=part)
                            p1_done = resid_gemm(bA_of, bA_keys, nfn)
                        elif final_norm:
                            fin_done = resid_gemm(bA_of, bA_keys, lambda lo, hi, part, s=s: final_norm_range(s, lo, hi, part))
                        else:
                            resid_gemm(bA_of, bA_keys, lambda lo, hi, part: None)
                    else:
                        gemm(8, bA_of, bA_keys, SEQ, resid_evac)
                    flush_sin()

            flush_late()
            if final_norm and not fin_done:
                final_norm_range(s, 0, SEQ)
            if not final_norm:
                for k in range(KC):
                    S.add("sp", lambda e: e.dma_start(out=outT[s, k], in_=hT[:, k, :]), reads=tk("h", k, 0, SEQ), stream=("hout", k, 0))
        stats = S.emit(final_streams=[st_ for st_ in S.stream_ops if st_[0] == "hout"])
    return nc, stats


def _chunk(Wsub):
    return np.ascontiguousarray(Wsub.reshape(KC, 128, 128).transpose(1, 0, 2))


def prep_weights(w_in, w_mix_out, w_xq, w_xkv, w_xo, w_ff1, w_ff2):
    out = np.empty((DEPTH * NCH, 128, KC, 128), np.float32)
    g = 0
    for l in range(DEPTH):
        cols = []
        for cc in range(4):
            cols += [2560 + 128 * cc, 2048 + 128 * cc, 1536 + 128 * cc]
        for hh in range(4):
            cols.append(1024 + 128 * hh)
        for hh in range(4):
            cols += [128 * hh, 512 + 128 * hh]
        for c0 in cols:
            out[g] = _chunk(w_in[l][:, c0:c0 + 128]); g += 1
        for c in range(8):
            out[g] = _chunk(w_mix_out[l][:, c * 128:(c + 1) * 128]); g += 1
        for c in range(8):
            out[g] = _chunk(w_xq[l][:, c * 128:(c + 1) * 128]); g += 1
            out[g] = _chunk(w_xkv[l][:, c * 128:(c + 1) * 128]); g += 1
            out[g] = _chunk(w_xkv[l][:, 1024 + c * 128:1024 + (c + 1) * 128]); g += 1
        for c in range(8):
            out[g] = _chunk(w_xo[l][:, c * 128:(c + 1) * 128]); g += 1
        for b in range(4):
            for c in range(8):
                out[g] = _chunk(w_ff1[l][:, b * 1024 + c * 128:b * 1024 + (c + 1) * 128]); g += 1
            for c in range(8):
                out[g] = _chunk(w_ff2[l][b * 1024:(b + 1) * 1024, c * 128:(c + 1) * 128]); g += 1
    assert g == DEPTH * NCH
    return out


def prep_consts(norm_mix_g, norm_x_g, norm_ffn_g, final_g, mem_norm_g, conv_w, subln_g):
    c = np.zeros((128, NCONST), np.float32)
    pk = lambda v: np.asarray(v, np.float32).reshape(KC, 128).T
    for l in range(DEPTH):
        c[:, C_GMIX + 8 * l:C_GMIX + 8 * l + 8] = pk(norm_mix_g[l])
        c[:, C_GX + 8 * l:C_GX + 8 * l + 8] = pk(norm_x_g[l])
        c[:, C_GFFN + 8 * l:C_GFFN + 8 * l + 8] = pk(norm_ffn_g[l])
        for cc in range(4):
            for j in range(3):
                c[:, C_CONVW + (l * 4 + cc) * 3 + j] = conv_w[l][j, cc * 128:(cc + 1) * 128]
    for l in range(DEPTH):
        c[:, C_GSUB + l] = subln_g[l]
    c[:, C_GFIN:C_GFIN + 8] = pk(final_g)
    c[:, C_GMEM:C_GMEM + 8] = pk(mem_norm_g)
    p = np.arange(128)
    r = p % 64
    invf = np.where(r < 16, 500000.0 ** (-(2.0 * (r % 8)) / 16.0), 0.0)
    hi = invf.astype(np.float32)
    c[:, C_INVF] = hi
    c[:, C_INVFLO] = (invf - hi.astype(np.float64)).astype(np.float32)
    c[:, C_IDENT:C_IDENT + 128] = np.eye(128, dtype=np.float32)
    kk, qq = np.meshgrid(p, p, indexing="ij")
    c[:, C_MASK:C_MASK + 128] = (qq >= kk).astype(np.float32)
    PT = np.zeros((128, 128), np.float32)
    for base in (0, 64):
        for i in range(8):
            PT[base + i + 8, base + i] = -1.0
            PT[base + i, base + i + 8] = 1.0
    c[:, C_PERM:C_PERM + 128] = PT
    return c


_PROG = {}


def kernel(x, mem, positions, norm_mix_g, w_in, lam_q1, lam_k1, lam_q2, lam_k2, subln_g, conv_w, w_mix_out,
           norm_x_g, mem_norm_g, w_xq, w_xkv, w_xo, norm_ffn_g, w_ff1, w_ff2, final_g):
    x = np.asarray(x, np.float32)
    mem = np.asarray(mem, np.float32)
    positions = np.asarray(positions, np.int32)
    f = lambda a: np.asarray(a, np.float32)
    wstream = prep_weights(f(w_in), f(w_mix_out), f(w_xq), f(w_xkv), f(w_xo), f(w_ff1), f(w_ff2))
    consts = prep_consts(f(norm_mix_g), f(norm_x_g), f(norm_ffn_g), f(final_g), f(mem_norm_g), f(conv_w), f(subln_g))
    lamg = np.zeros((1, DEPTH * LAMG), np.float32)
    for l in range(DEPTH):
        lamg[0, l * LAMG:(l + 1) * LAMG] = np.concatenate([f(lam_q1)[l], f(lam_k1)[l], f(lam_q2)[l], f(lam_k2)[l], f(subln_g)[l]])
    if "nc" not in _PROG:
        _PROG["nc"] = build_program()[0]
    nc = _PROG["nc"]
    in_maps = []
    for c in range(8):
        xs = x[2 * c:2 * c + 2]
        xTc = np.ascontiguousarray(xs.transpose(0, 2, 1)).reshape(NSEQ, KC, 128, SEQ)
        ms = mem[2 * c:2 * c + 2]
        mTc = np.ascontiguousarray(ms.transpose(0, 2, 1).reshape(NSEQ, KC, 128, MEM).transpose(0, 2, 1, 3)).reshape(NSEQ, 128, KC * MEM)
        in_maps.append(dict(xT=xTc, memT=mTc, pos=np.ascontiguousarray(positions[2 * c:2 * c + 2]),
                            wst=wstream, consts=consts, lamg=lamg))
    res = run_bass_kernel_spmd(nc, in_maps, core_ids=list(range(8)))
    out = np.empty((16, SEQ, D), np.float32)
    for c in range(8):
        o = np.asarray(res.results[c]["outT"]).reshape(NSEQ, D, SEQ)
        out[2 * c:2 * c + 2] = o.transpose(0, 2, 1)
    return out
```
